# Optimizing a Trainium2 kernel written in Bass

```python
import math
import jax, jax.numpy as jnp
from jax import lax
import numpy as np

D_MODEL = 1024
BATCH = 2
SEQ = 8192
DEPTH = 1
DEC_BATCH = 8
DEC_SEQ = 16
PAST_LEN = 2048

CHUNK = 64
Q_BLOCK = 128
EPS = 1e-6
NEG_INF = -1e30
ROPE_THETA = 10000.0
MLA_HEADS = 8
Q_LORA = 768
KV_LORA = 512
QK_NOPE = 128
QK_ROPE = 64
V_HEAD = 128
MLA_SCALE = (QK_NOPE + QK_ROPE) ** -0.5
MLA_WIDTH = MLA_HEADS * V_HEAD
RET_HEADS = 4
RET_DK = 128
RET_DV = 256
RET_WIDTH = RET_HEADS * RET_DV
D_FF = 2816
CONV_W = 3
IN_SIZES = (Q_LORA, KV_LORA, QK_ROPE, RET_HEADS * RET_DK, RET_HEADS * RET_DK,
            RET_WIDTH, RET_WIDTH, D_MODEL, D_MODEL)
IN_WIDTH = Q_LORA + KV_LORA + QK_ROPE + 2 * RET_HEADS * RET_DK + 2 * RET_WIDTH + 2 * D_MODEL

kernel_name = "hybrid_mla_retention_convffn_stream_step"


def rms_norm(x, gain):
    xf = x.astype(jnp.float32)
    inv = lax.rsqrt(jnp.mean(xf * xf, axis=-1, keepdims=True) + EPS)
    return (xf * inv * gain.astype(jnp.float32)).astype(x.dtype)


def rope(x, pos):
    half = x.shape[-1] // 2
    freqs = ROPE_THETA ** (-jnp.arange(half, dtype=jnp.float32) / half)
    ang = pos.astype(jnp.float32)[:, None] * freqs[None, :]
    cos = jnp.cos(ang)[:, None, :]
    sin = jnp.sin(ang)[:, None, :]
    xf = x.astype(jnp.float32)
    x1, x2 = xf[..., :half], xf[..., half:]
    out = jnp.concatenate([x1 * cos - x2 * sin, x1 * sin + x2 * cos], axis=-1)
    return out.astype(x.dtype)


def retention_log_decay():
    return jnp.log1p(-(2.0 ** (-5.0 - jnp.arange(RET_HEADS, dtype=jnp.float32))))


def retention_chunk(q, k, v, S, lg):
    L = q.shape[1]
    idx = jnp.arange(L, dtype=jnp.float32)
    diff = idx[:, None] - idx[None, :]
    decay = jnp.where(diff[None] >= 0,
                      jnp.exp(jnp.maximum(diff, 0.0)[None] * lg[:, None, None]), 0.0)
    s = jnp.einsum('bihd,bjhd->bhij', q, k) * decay[None]
    inner = jnp.einsum('bhij,bjhe->bihe', s, v)
    q_dec = jnp.exp((idx + 1.0)[:, None] * lg[None, :])[None, :, :, None]
    cross = jnp.einsum('bihd,bhde->bihe', q, S) * q_dec
    k_dec = k * jnp.exp((L - 1.0 - idx)[:, None] * lg[None, :])[None, :, :, None]
    S_new = S * jnp.exp(L * lg)[None, :, None, None] + jnp.einsum('bjhd,bjhe->bhde', k_dec, v)
    return inner + cross, S_new


def head_group_norm(y, gain):
    mu = jnp.mean(y, axis=-1, keepdims=True)
    yc = y - mu
    var = jnp.mean(yc * yc, axis=-1, keepdims=True)
    return yc * lax.rsqrt(var + EPS) * gain.astype(jnp.float32)


def mla_expand(c_kv, w_kv_up, g_k_nope):
    B, T = c_kv.shape[:2]
    kv = (c_kv @ w_kv_up).reshape(B, T, MLA_HEADS, QK_NOPE + V_HEAD)
    return rms_norm(kv[..., :QK_NOPE], g_k_nope), kv[..., QK_NOPE:]


def mla_scores(qn, qp, k_nope, k_pe):
    s = jnp.einsum('bqhd,bkhd->bhqk', qn, k_nope) + jnp.einsum('bqhr,bkr->bhqk', qp, k_pe)
    return s.astype(jnp.float32) * MLA_SCALE


def mla_prompt_attention(q_nope, q_pe, k_nope, k_pe, v):
    B, T = q_nope.shape[:2]
    nb = T // Q_BLOCK
    qn = jnp.moveaxis(q_nope.reshape(B, nb, Q_BLOCK, MLA_HEADS, QK_NOPE), 1, 0)
    qp = jnp.moveaxis(q_pe.reshape(B, nb, Q_BLOCK, MLA_HEADS, QK_ROPE), 1, 0)
    k_chunk = jnp.arange(T) // CHUNK

    def block(args):
        qn_b, qp_b, i = args
        q_chunk = (i * Q_BLOCK + jnp.arange(Q_BLOCK)) // CHUNK
        s = mla_scores(qn_b, qp_b, k_nope, k_pe)
        s = jnp.where((k_chunk[None, :] <= q_chunk[:, None])[None, None], s, NEG_INF)
        p = jax.nn.softmax(s, axis=-1).astype(v.dtype)
        return jnp.einsum('bhqk,bkhd->bqhd', p, v)

    o = lax.map(block, (qn, qp, jnp.arange(nb)))
    return jnp.moveaxis(o, 0, 1).reshape(B, T, MLA_WIDTH)


def mla_sample_attention(q_nope, q_pe, k_nope, k_pe, v):
    B, T = q_nope.shape[:2]
    p = jax.nn.softmax(mla_scores(q_nope, q_pe, k_nope, k_pe), axis=-1).astype(v.dtype)
    return jnp.einsum('bhqk,bkhd->bqhd', p, v).reshape(B, T, MLA_WIDTH)


def hybrid_layer(x, pos, past, w):
    B, T, _ = x.shape
    h = rms_norm(x, w['g_norm_mix'])
    z = h @ w['w_in']
    offs = [int(o) for o in np.cumsum(IN_SIZES)[:-1]]
    q_lat, kv_lat, k_pe, rq, rk, rv, rg, ga, gb = jnp.split(z, offs, axis=-1)

    q = (rms_norm(q_lat, w['g_q_lat']) @ w['w_q_up']).reshape(B, T, MLA_HEADS, QK_NOPE + QK_ROPE)
    q_nope = rms_norm(q[..., :QK_NOPE], w['g_q_nope'])
    q_pe = rope(rms_norm(q[..., QK_NOPE:], w['g_q_rope']), pos)
    c_kv = rms_norm(kv_lat, w['g_kv_lat'])
    k_pe = rope(rms_norm(k_pe, w['g_k_rope'])[:, :, None, :], pos)[:, :, 0, :]
    if past is None:
        k_nope, v = mla_expand(c_kv, w['w_kv_up'], w['g_k_nope'])
        attn = mla_prompt_attention(q_nope, q_pe, k_nope, k_pe, v)
    else:
        lat_all = jnp.concatenate([past[0].astype(c_kv.dtype), c_kv], axis=1)
        pe_all = jnp.concatenate([past[1].astype(k_pe.dtype), k_pe], axis=1)
        k_nope, v = mla_expand(lat_all, w['w_kv_up'], w['g_k_nope'])
        attn = mla_sample_attention(q_nope, q_pe, k_nope, pe_all, v)

    lg = retention_log_decay()
    rq = rope(rq.reshape(B, T, RET_HEADS, RET_DK), pos).astype(jnp.float32)
    rk = rope(rk.reshape(B, T, RET_HEADS, RET_DK), pos).astype(jnp.float32) * (RET_DK ** -0.5)
    rv = rv.reshape(B, T, RET_HEADS, RET_DV).astype(jnp.float32)
    if past is None:
        nc = T // CHUNK

        def to_chunks(a):
            return jnp.swapaxes(a.reshape(B, nc, CHUNK, *a.shape[2:]), 0, 1)

        def step(S, xs):
            o, S = retention_chunk(xs[0], xs[1], xs[2], S, lg)
            return S, o

        S0 = jnp.zeros((B, RET_HEADS, RET_DK, RET_DV), jnp.float32)
        S_fin, o = lax.scan(step, S0, (to_chunks(rq), to_chunks(rk), to_chunks(rv)))
        ret = jnp.swapaxes(o, 0, 1).reshape(B, T, RET_HEADS, RET_DV)
    else:
        ret, S_fin = retention_chunk(rq, rk, rv, past[2].astype(jnp.float32), lg)
    ret = head_group_norm(ret, w['g_ret_out']).reshape(B, T, RET_WIDTH)
    ret = (jax.nn.silu(rg.astype(jnp.float32)) * ret).astype(x.dtype)

    a_d = attn @ w['w_o_branch'][:MLA_WIDTH]
    r_d = ret @ w['w_o_branch'][MLA_WIDTH:]
    merged = jax.nn.sigmoid(ga) * a_d + jax.nn.sigmoid(gb) * r_d
    x = x + merged @ w['w_out']

    u = rms_norm(x, w['g_norm_ffn']) @ w['w_ffn_up']
    if past is None:
        prev = jnp.zeros((B, CONV_W - 1, 2 * D_FF), u.dtype)
    else:
        prev = past[3].astype(u.dtype)
    up = jnp.concatenate([prev, u], axis=1)
    cw = w['ffn_conv_w']
    c = up[:, 0:T] * cw[0] + up[:, 1:T + 1] * cw[1] + up[:, 2:T + 2] * cw[2] + w['ffn_conv_b']
    gate, val = c[..., :D_FF], c[..., D_FF:]
    x = x + (jax.nn.silu(gate) * val) @ w['w_ffn_down']
    new_state = (c_kv, k_pe, S_fin.astype(x.dtype), up[:, -(CONV_W - 1):])
    return x, new_state


def setup_inputs(seed: int = 0) -> dict:
    key = jax.random.key(seed)
    ks = jax.random.split(key, 32)
    f32 = jnp.float32
    nrm = lambda k, shape, scale: jax.random.normal(k, shape, f32) * scale
    gain = lambda k, n: 1.0 + 0.1 * jax.random.normal(k, (DEPTH,) + n, f32)
    return {
        'x_prompt': nrm(ks[0], (BATCH, SEQ, D_MODEL), 1.0),
        'x_sample': nrm(ks[1], (DEC_BATCH, DEC_SEQ, D_MODEL), 1.0),
        'cache_mla_latent': nrm(ks[2], (DEPTH, DEC_BATCH, PAST_LEN, KV_LORA), 1.0),
        'cache_mla_rope_key': nrm(ks[3], (DEPTH, DEC_BATCH, PAST_LEN, QK_ROPE), 1.0),
        'state_retention': nrm(ks[4], (DEPTH, DEC_BATCH, RET_HEADS, RET_DK, RET_DV), 1.0),
        'state_ffn_conv': nrm(ks[5], (DEPTH, DEC_BATCH, CONV_W - 1, 2 * D_FF), 1.0),
        'g_norm_mix': gain(ks[6], (D_MODEL,)),
        'w_in': nrm(ks[7], (DEPTH, D_MODEL, IN_WIDTH), D_MODEL ** -0.5),
        'g_q_lat': gain(ks[8], (Q_LORA,)),
        'w_q_up': nrm(ks[9], (DEPTH, Q_LORA, MLA_HEADS * (QK_NOPE + QK_ROPE)), Q_LORA ** -0.5),
        'g_q_nope': gain(ks[10], (QK_NOPE,)),
        'g_q_rope': gain(ks[11], (QK_ROPE,)),
        'g_kv_lat': gain(ks[12], (KV_LORA,)),
        'w_kv_up': nrm(ks[13], (DEPTH, KV_LORA, MLA_HEADS * (QK_NOPE + V_HEAD)), KV_LORA ** -0.5),
        'g_k_nope': gain(ks[14], (QK_NOPE,)),
        'g_k_rope': gain(ks[15], (QK_ROPE,)),
        'g_ret_out': gain(ks[16], (RET_HEADS, RET_DV)),
        'w_o_branch': nrm(ks[17], (DEPTH, MLA_WIDTH + RET_WIDTH, D_MODEL), MLA_WIDTH ** -0.5),
        'w_out': nrm(ks[18], (DEPTH, D_MODEL, D_MODEL), D_MODEL ** -0.5),
        'g_norm_ffn': gain(ks[19], (D_MODEL,)),
        'w_ffn_up': nrm(ks[20], (DEPTH, D_MODEL, 2 * D_FF), D_MODEL ** -0.5),
        'ffn_conv_w': nrm(ks[21], (DEPTH, CONV_W, 2 * D_FF), CONV_W ** -0.5),
        'ffn_conv_b': nrm(ks[22], (DEPTH, 2 * D_FF), 0.01),
        'w_ffn_down': nrm(ks[23], (DEPTH, D_FF, D_MODEL), D_FF ** -0.5),
    }


def reference(x_prompt, x_sample, cache_mla_latent, cache_mla_rope_key, state_retention,
              state_ffn_conv, g_norm_mix, w_in, g_q_lat, w_q_up, g_q_nope, g_q_rope,
              g_kv_lat, w_kv_up, g_k_nope, g_k_rope, g_ret_out, w_o_branch, w_out,
              g_norm_ffn, w_ffn_up, ffn_conv_w, ffn_conv_b, w_ffn_down):
    pos_p = jnp.arange(x_prompt.shape[1])
    pos_s = PAST_LEN + jnp.arange(x_sample.shape[1])
    yp, ys = x_prompt, x_sample
    st_p, st_s = [], []
    for l in range(DEPTH):
        wl = dict(g_norm_mix=g_norm_mix[l], w_in=w_in[l], g_q_lat=g_q_lat[l], w_q_up=w_q_up[l],
                  g_q_nope=g_q_nope[l], g_q_rope=g_q_rope[l], g_kv_lat=g_kv_lat[l],
                  w_kv_up=w_kv_up[l], g_k_nope=g_k_nope[l], g_k_rope=g_k_rope[l],
                  g_ret_out=g_ret_out[l], w_o_branch=w_o_branch[l], w_out=w_out[l],
                  g_norm_ffn=g_norm_ffn[l], w_ffn_up=w_ffn_up[l], ffn_conv_w=ffn_conv_w[l],
                  ffn_conv_b=ffn_conv_b[l], w_ffn_down=w_ffn_down[l])
        yp, sp = hybrid_layer(yp, pos_p, None, wl)
        past = (cache_mla_latent[l], cache_mla_rope_key[l], state_retention[l], state_ffn_conv[l])
        ys, ss = hybrid_layer(ys, pos_s, past, wl)
        st_p.append(sp)
        st_s.append(ss)
    stk = lambda lst, i: jnp.stack([s[i] for s in lst], axis=0)
    return (yp, ys, stk(st_p, 0), stk(st_p, 1), stk(st_p, 2), stk(st_p, 3),
            stk(st_s, 0), stk(st_s, 1), stk(st_s, 2), stk(st_s, 3))
```

```python
import contextlib
import numpy as np
import ml_dtypes
import concourse.bass as bass
import concourse.mybir as mybir
from concourse.bass_utils import run_bass_kernel_spmd

F32 = mybir.dt.float32
BF16 = mybir.dt.bfloat16
ALU = mybir.AluOpType
AF = mybir.ActivationFunctionType
AX = mybir.AxisListType

D = 1024
EPS = 1e-6
THETA = 10000.0
NH = 8
QL = 768
KVL = 512
RH = 4
DFF = 2816
NCH = 44
PAST = 2048
MLA_SCALE = 192 ** -0.5
NEG = -30000.0
COMPUTE = ("pe", "act", "dve", "pool")


class Buf:
    __slots__ = ("writers", "readers")

    def __init__(self):
        self.writers = []
        self.readers = []


class Op:
    __slots__ = ("eng", "fn", "deps", "dma", "pos", "idx", "sig", "semval", "dsem", "dval", "guard", "epoch")


class Prog:
    def __init__(self, nc, st, nds=48):
        self.nc = nc
        self.nds = nds
        self.esem = {e: st.enter_context(nc.semaphore("s_" + e)) for e in COMPUTE}
        self.dsem = [st.enter_context(nc.semaphore("d_%d" % i)) for i in range(nds)]
        self.bar = st.enter_context(nc.semaphore("s_bar"))
        self.ecnt = {e: 0 for e in COMPUTE}
        self.last_use = [None] * nds
        self.use_cnt = [0] * nds
        self.k = 0
        self.k2 = 0
        self.epoch = 0
        self.nbar = 0
        self.gidx = 0
        self._reset()

    def _reset(self):
        self.ops = []
        self.streams = {e: [] for e in ("pe", "act", "dve", "pool", "sp")}

    def _add(self, eng, fn, reads, writes, dma):
        op = Op()
        op.eng, op.fn, op.dma = eng, fn, dma
        op.sig, op.semval, op.dsem, op.dval, op.guard = False, 0, None, 0, None
        op.epoch = self.epoch
        op.idx = self.gidx
        self.gidx += 1
        op.pos = len(self.streams[eng])
        deps = []
        for b in reads:
            deps.extend(b.writers)
        for b in writes:
            deps.extend(b.writers)
            deps.extend(b.readers)
        seen = set()
        op.deps = []
        for d in deps:
            if d.epoch != self.epoch or d.idx in seen:
                continue
            seen.add(d.idx)
            op.deps.append(d)
        for b in writes:
            b.writers = [op]
            b.readers = []
        for b in reads:
            b.readers.append(op)
        self.ops.append(op)
        self.streams[eng].append(op)
        return op

    def op(self, eng, fn, reads=(), writes=()):
        return self._add(eng, fn, reads, writes, False)

    def dma(self, fn, reads=(), writes=(), eng="sp"):
        return self._add(eng, fn, reads, writes, True)

    def emit_stage(self):
        nc = self.nc
        waited = {e: {f: -1 for f in COMPUTE} for e in self.streams}
        waited_dma = {e: set() for e in self.streams}
        waits = {}
        for op in self.ops:
            e = op.eng
            wl = []
            for d in op.deps:
                if d.dma:
                    if d.idx in waited_dma[e]:
                        continue
                    waited_dma[e].add(d.idx)
                    wl.append(d)
                else:
                    f = d.eng
                    if f == e:
                        if e == "pe":
                            continue
                    if waited[e][f] >= d.pos:
                        continue
                    waited[e][f] = d.pos
                    wl.append(d)
            waits[op.idx] = wl
            for d in wl:
                d.sig = True
        for e in COMPUTE:
            cops = [o for o in self.streams[e] if not o.dma]
            if cops:
                cops[-1].sig = True
            for op in self.streams[e]:
                if not op.dma and op.sig:
                    self.ecnt[e] += 1
                    op.semval = self.ecnt[e]
        for op in self.ops:
            if op.dma:
                if op.eng == "sp":
                    s = self.k % (self.nds - 16)
                    self.k += 1
                else:
                    s = self.nds - 16 + (self.k2 % 16)
                    self.k2 += 1
                op.dsem = s
                op.guard = self.last_use[s]
                self.use_cnt[s] += 1
                op.dval = 16 * self.use_cnt[s]
                self.last_use[s] = op
        self.nbar += 1
        nbar = self.nbar
        esem, dsem, bar = self.esem, self.dsem, self.bar
        finals = dict(self.ecnt)
        dfinals = [16 * c for c in self.use_cnt]

        def wait_on(eo, d):
            if d.dma:
                eo.wait_ge(dsem[d.dsem], d.dval)
            else:
                eo.wait_ge(esem[d.eng], d.semval)

        streams = self.streams
        with nc.Block() as blk:
            deco = {"pe": blk.tensor, "act": blk.scalar, "dve": blk.vector, "pool": blk.gpsimd, "sp": blk.sync}

            def make(e, stream):
                def body(eo):
                    for op in stream:
                        for d in waits[op.idx]:
                            wait_on(eo, d)
                        if op.dma and op.guard is not None:
                            eo.wait_ge(dsem[op.guard.dsem], op.guard.dval)
                        ins = op.fn()
                        if op.dma:
                            ins.then_inc(dsem[op.dsem], 16)
                        elif op.sig:
                            ins.then_inc(esem[e], 1)
                    for f in COMPUTE:
                        if f != e and finals[f] > 0:
                            eo.wait_ge(esem[f], finals[f])
                    if e == "sp":
                        for i in range(self.nds):
                            if dfinals[i] > 0:
                                eo.wait_ge(dsem[i], dfinals[i])
                        eo.sem_inc(bar, 1)
                    else:
                        eo.wait_ge(bar, nbar)
                return body

            for e in ("sp", "pe", "act", "dve", "pool"):
                deco[e](make(e, streams[e]))
        self.epoch += 1
        self._reset()


class Rot:
    def __init__(self, items):
        self.items = items
        self.i = 0

    def next(self):
        it = self.items[self.i % len(self.items)]
        self.i += 1
        return it


def bc(ap, shape, axis):
    return ap.unsqueeze(axis).to_broadcast(shape)


def build(NO, debug=False):
    NT = 16 * NO
    NOWN = 4 * NO
    UH = NOWN
    US = NOWN + 1
    NU = NOWN + 2
    NQ = 128 * NU
    NSL = NT + 1
    nc = bass.Bass("TRN2", target_bir_lowering=False)

    def din(name, shape, dt=F32):
        return nc.dram_tensor(name, list(shape), dt, kind="ExternalInput").ap()

    def dout(name, shape, dt=F32):
        return nc.dram_tensor(name, list(shape), dt, kind="ExternalOutput").ap()

    def dscr(name, shape, dt=BF16):
        return nc.dram_tensor(name, list(shape), dt, kind="ExternalOutput" if debug else "Internal").ap()

    xall = din("xall", [NT, 128, D])
    taball = din("taball", [NT, 128, 192])
    xH = din("xH", [128, D]); tabH = din("tabH", [128, 192])
    xH2 = din("xH2", [128, D]); tabH2 = din("tabH2", [128, 192])
    xS = din("xS", [128, D]); tabS = din("tabS", [128, 192])
    latp = din("latp", [16, 128, KVL]); kpep = din("kpep", [16, 128, 64])
    S0 = din("S0", [RH, 128, 256]); conv0 = din("conv0", [2, 2 * DFF])
    w_in = din("w_in", [D, 6464]); w_q_up = din("w_q_up", [QL, 1536]); w_kv_up = din("w_kv_up", [KVL, 2048])
    w_ob = din("w_ob", [2048, D]); w_out = din("w_out", [D, D])
    w_up = din("w_up", [D, 2 * DFF]); w_dn = din("w_dn", [DFF, D])
    cw = din("cw", [3, 2 * DFF]); cb = din("cb", [1, 2 * DFF])
    g_mix = din("g_mix", [1, D]); g_ql = din("g_ql", [1, QL]); g_qn = din("g_qn", [1, 128]); g_qr = din("g_qr", [1, 64])
    g_kv = din("g_kv", [1, KVL]); g_kn = din("g_kn", [1, 128]); g_kr = din("g_kr", [1, 64])
    g_ret = din("g_ret", [1, 1024]); g_ffn = din("g_ffn", [1, D])
    ident_d = din("ident", [128, 128], BF16); ident32_d = din("ident32", [128, 128])
    cmask_d = din("cmask", [128, 128], BF16)
    DT_d = din("DT", [128, 4 * 128]); qdec_d = din("qdec", [128, 4 * 128])
    coef_d = din("coef", [128, 256])
    maskH_d = din("maskH", [128, NT * 8], BF16)

    yp = dout("yp", [NOWN, 128, D]); ys = dout("ys", [128, D])
    latn = dout("latn", [NOWN, 128, KVL]); kpen = dout("kpen", [NOWN, 128, 64])
    retp = dout("retp", [RH, 128, 256]); convp = dout("convp", [NCH, 128, 2])
    latS = dout("latS", [128, KVL]); kpeS = dout("kpeS", [128, 64])
    retS = dout("retS", [RH, 128, 256]); convS = dout("convS", [NCH, 128, 2])

    KT = dscr("KT", [NH, 128, NSL * 128]); KPE = dscr("KPE", [64, NSL * 128]); VV = dscr("VV", [NH, NSL, 128, 128])
    KTs = dscr("KTs", [NH, 128, 17 * 128]); KPEs = dscr("KPEs", [64, 17 * 128]); VVs = dscr("VVs", [NH, 17, 128, 128])
    QT = dscr("QT", [NH, 128, NQ]); QPE = dscr("QPE", [NH, 64, NQ])
    RETT = dscr("RETT", [8, 128, NQ]); GATE = dscr("GATE", [NU, 128, 2048])
    ATT = dscr("ATT", [NH, 128, NQ])
    XMID = dscr("XMID", [NU, 128, D], F32); HNT = dscr("HNT", [8, 128, NQ])

    C_CO, C_CN, C_CNO, C_SO, C_G2048, C_G128, C_CS = 0, 48, 96, 112, 116, 120, 124
    C_KD, C_NINVG, C_FLAG, C_SEL, C_HF, C_VB, C_VBS = 128, 132, 136, 140, 148, 160, 172

    with contextlib.ExitStack() as top:
        P = Prog(nc, top)

        def mm(out, lhsT, rhs, start, stop, R, W):
            return P.op("pe", lambda: nc.tensor.matmul(out, lhsT=lhsT, rhs=rhs, start=start, stop=stop), R, W)

        def tr(out, in_, idn, R, W):
            return P.op("pe", lambda: nc.tensor.transpose(out, in_, idn), R, W)

        def act(out, in_, func, R, W, **kw):
            return P.op("act", lambda: nc.scalar.activation(out=out, in_=in_, func=func, **kw), R, W)

        def tt(out, in0, in1, op, R, W, eng="dve"):
            e = nc.vector if eng == "dve" else nc.gpsimd
            return P.op(eng, lambda: e.tensor_tensor(out=out, in0=in0, in1=in1, op=op), R, W)

        def ts(out, in0, s1, s2, op0, op1, R, W, eng="dve"):
            e = nc.vector if eng == "dve" else nc.gpsimd
            if s2 is None:
                return P.op(eng, lambda: e.tensor_scalar(out=out, in0=in0, scalar1=s1, scalar2=None, op0=op0), R, W)
            return P.op(eng, lambda: e.tensor_scalar(out=out, in0=in0, scalar1=s1, scalar2=s2, op0=op0, op1=op1), R, W)

        def stt(out, in0, sc, in1, op0, op1, R, W, eng="dve"):
            e = nc.vector if eng == "dve" else nc.gpsimd
            return P.op(eng, lambda: e.scalar_tensor_tensor(out=out, in0=in0, scalar=sc, in1=in1, op0=op0, op1=op1), R, W)

        def red(out, in_, R, W):
            return P.op("dve", lambda: nc.vector.tensor_reduce(out=out, in_=in_, axis=AX.X, op=ALU.add), R, W)

        def cp(out, in_, R, W, eng="dve"):
            if eng == "act":
                return P.op("act", lambda: nc.scalar.copy(out=out, in_=in_), R, W)
            e = nc.vector if eng == "dve" else nc.gpsimd
            return P.op(eng, lambda: e.tensor_copy(out=out, in_=in_), R, W)

        def recip(out, in_, R, W):
            return P.op("dve", lambda: nc.vector.reciprocal(out=out, in_=in_), R, W)

        def dma(out, in_, R, W):
            return P.dma(lambda: nc.sync.dma_start(out=out, in_=in_), R, W)

        def dmas(out, in_, R, W):
            return P.dma(lambda: nc.gpsimd.dma_start(out=out, in_=in_), R, W, eng="pool")

        def rstd(ss_ap, n, R, W):
            act(ss_ap, ss_ap, AF.Sqrt, R, W, scale=1.0 / n, bias=EPS)
            recip(ss_ap, ss_ap, R, W)

        def rope(dst, src, cos, sin, H, half, tmp, R, W):
            shp = [128, H, half]
            c = bc(cos, shp, 1)
            s = bc(sin, shp, 1)
            x1, x2 = src[:, :, 0:half], src[:, :, half:2 * half]
            t1, t2 = tmp[:, :, 0:half], tmp[:, :, half:2 * half]
            tt(t1, x1, c, ALU.mult, R, W)
            tt(t2, x2, s, ALU.mult, R, W)
            tt(dst[:, :, 0:half], t1, t2, ALU.subtract, R, W)
            tt(t1, x1, s, ALU.mult, R, W)
            tt(t2, x2, c, ALU.mult, R, W)
            tt(dst[:, :, half:2 * half], t1, t2, ALU.add, R, W)

        with contextlib.ExitStack() as st:
            cnt = [0]

            def sb(shape, dt, n=None):
                items = []
                for _ in range(n or 1):
                    cnt[0] += 1
                    items.append((st.enter_context(nc.sbuf_tensor("a%d" % cnt[0], list(shape), dt)), Buf()))
                return items[0] if n is None else Rot(items)

            banks = Rot([(st.enter_context(nc.psum_tensor("pa%d" % i, [128, 512], F32)), Buf()) for i in range(8)])

            Win, bWin = sb([128, 8, 3648], BF16)
            Wkv, bWkv = sb([128, 4, 2048], BF16)
            gcol, bgcol = sb([128, 24], F32)
            ident, bid = sb([128, 128], BF16)
            coef, bcoef = sb([128, 256], F32)
            DTt, bDT = sb([128, 4, 128], F32)
            qdect, bqdec = sb([128, 4, 128], F32)
            gkv_r, bgkv = sb([128, KVL], F32)
            gkn_r, bgkn = sb([128, 128], F32)
            gkr_r, bgkr = sb([128, 64], F32)
            gret_r, bgret = sb([128, 1024], F32)

            dma(ident[:], ident_d, [], [bid])
            dma(coef[:], coef_d, [], [bcoef])
            dma(DTt[:], DT_d.rearrange("p (h n) -> p h n", h=4), [], [bDT])
            dma(qdect[:], qdec_d.rearrange("p (h n) -> p h n", h=4), [], [bqdec])
            dma(gkv_r[:], g_kv.partition_broadcast(128), [], [bgkv])
            dma(gkn_r[:], g_kn.partition_broadcast(128), [], [bgkn])
            dma(gkr_r[:], g_kr.partition_broadcast(128), [], [bgkr])
            dma(gret_r[:], g_ret.partition_broadcast(128), [], [bgret])
            P.dma(lambda: nc.sync.dma_start(out=gcol[:, 0:8], in_=g_mix.rearrange("o (c p) -> p (o c)", p=128),
                                            allow_slow_non_contiguous=True), [], [bgcol])
            P.dma(lambda: nc.sync.dma_start(out=gcol[:, 8:14], in_=g_ql.rearrange("o (c p) -> p (o c)", p=128),
                                            allow_slow_non_contiguous=True), [], [bgcol])

            tog = [0]

            def load_w(dst, bdst, src, KC, N, gc0, gR):
                for c in range(KC):
                    for n0 in range(0, N, 2048):
                        n1 = min(N, n0 + 2048)
                        s_, bs_ = stg.next()
                        dma(s_[:, 0:n1 - n0], src[c * 128:(c + 1) * 128, n0:n1], [], [bs_])
                        tog[0] += 1
                        if gc0 is None:
                            cp(dst[:, c, n0:n1], s_[:, 0:n1 - n0], [bs_], [], eng="act" if tog[0] % 2 else "dve")
                        elif tog[0] % 2:
                            act(dst[:, c, n0:n1], s_[:, 0:n1 - n0], AF.Identity, [bs_] + gR, [], scale=gcol[:, gc0 + c:gc0 + c + 1])
                        else:
                            ts(dst[:, c, n0:n1], s_[:, 0:n1 - n0], gcol[:, gc0 + c:gc0 + c + 1], None, ALU.mult, None, [bs_] + gR, [])
                cp(stg.items[0][0][:, 0:1], stg.items[0][0][:, 0:1], [], [it[1] for it in stg.items] + [bdst], eng="dve")

            with contextlib.ExitStack() as st2:
                stg = Rot([(st2.enter_context(nc.sbuf_tensor("astg%d" % i, [128, 2048], F32)), Buf()) for i in range(10)])
                load_w(Win, bWin, w_in[:, 768:4416], 8, 3648, 0, [bgcol])
                load_w(Wkv, bWkv, w_kv_up, 4, 2048, None, [])
                P.emit_stage()

            xt_r = sb([128, D], F32, 2)
            tab_r = sb([128, 192], F32, 2)
            junk = st.enter_context(nc.sbuf_tensor("junk", [128, D], BF16))
            sq_r = sb([128, 512], F32, 2)
            sm_r = sb([128, 64], F32, 4)
            xn_r = sb([128, D], BF16, 2)
            hT_r = sb([128, 8, 128], BF16, 2)
            ckv32_r = sb([128, KVL], F32, 1)
            ckvb_r = sb([128, KVL], BF16, 2)
            ckvT_r = sb([128, 4, 128], BF16, 2)
            kp_r = sb([128, 2, 64], F32, 2)
            tmp_r = sb([128, 1024], F32, 1)
            kpb_r = sb([128, 64], BF16, 2)
            kpT_r = sb([64, 128], BF16, 2)
            vb_r = sb([128, 8, 128], BF16, 1)
            knb_r = sb([128, 8, 128], BF16, 2)
            ktT_r = sb([128, 8, 128], BF16, 1)
            rkr_r = sb([128, 4, 128], F32, 1)
            kdec_r = sb([128, 4, 128], BF16, 2)
            rks_r = sb([128, 4, 128], BF16, 2)
            rvb_r = sb([128, 4, 256], BF16, 2)
            rqr_r = sb([128, 4, 128], BF16, 2)
            srg_r = sb([128, 1024], F32, 2)
            rqT_r = sb([128, 4, 128], BF16, 1)
            rqdT_r = sb([128, 4, 128], BF16, 1)
            rkT_r = sb([128, 4, 128], BF16, 1)
            scm_r = sb([128, 4, 128], BF16, 1)
            y32_r = sb([128, 4, 256], F32, 1)
            retg_r = sb([128, 1024], BF16, 1)
            retT_r = sb([128, 8, 128], BF16, 1)
            S_oct, bS_oct = sb([128, 4, 256], F32)
            acc_own, bacc_own = sb([128, 4, 256], F32)
            acc_next, bacc_next = sb([128, 4, 256], F32)
            accS, baccS = sb([128, 4, 256], F32)
            Sb_r = sb([128, 4, 256], BF16, 2)
            accH, baccH = sb([128, 4, 256], F32)
            H_rq32, bH_rq32 = sb([128, 4, 128], F32)
            H_rqT, bH_rqT = sb([128, 4, 128], BF16)
            H2_rk32, bH2_rk32 = sb([128, 4, 128], F32)
            H2_rv32, bH2_rv32 = sb([128, 4, 256], F32)
            H_srg, bH_srg = sb([128, 1024], F32)

            P.op("dve", lambda: nc.vector.memset(S_oct[:], 0.0), [], [bS_oct])
            P.op("dve", lambda: nc.vector.memset(accH[:], 0.0), [], [baccH])

            def bank():
                t, b = banks.next()
                return t, b

            def front_a(x_src, tab_src):
                xt, bxt = xt_r.next()
                dma(xt[:], x_src, [], [bxt])
                tb, btb = tab_r.next()
                dma(tb[:], tab_src, [], [btb])
                sm, bsm = sm_r.next()
                act(junk[:], xt[:], AF.Square, [bxt], [bsm], accum_out=sm[:, 0:1])
                rstd(sm[:, 0:1], D, [bsm], [bsm])
                xn, bxn = xn_r.next()
                ts(xn[:], xt[:], sm[:, 0:1], None, ALU.mult, None, [bxt, bsm], [bxn])
                return xn, bxn, tb, btb

            def front_b(xn, bxn, tb, btb):
                pb, bpb = bank()
                pv = pb[:].bitcast(BF16).rearrange("p (c n) -> p c n", c=8)
                for c in range(8):
                    tr(pv[:, c, :], xn[:, c * 128:(c + 1) * 128], ident[:], [bxn, bid], [bpb])
                hT, bhT = hT_r.next()
                cp(hT[:], pv, [bpb], [bhT], eng="act")
                return hT, bhT, tb, btb

            def front(x_src, tab_src):
                return front_b(*front_a(x_src, tab_src))

            def zgroup(hT, bhT, c0, n):
                pb, bpb = bank()
                for c in range(8):
                    mm(pb[:, 0:n], hT[:, c, :], Win[:, c, c0 - 768:c0 - 768 + n], c == 0, c == 7, [bhT], [bpb])
                return pb, bpb

            def kside0(ckvb, bckvb, kpb, bkpb, KTd, KPEd, VVd, slot):
                pb, bpb = bank()
                pv = pb[:].bitcast(BF16)
                tr(pv[0:64, 0:128], kpb[:], ident[:], [bkpb, bid], [bpb])
                kpT, bkpT = kpT_r.next()
                cp(kpT[:], pv[0:64, 0:128], [bpb], [bkpT], eng="act")
                dmas(KPEd[:, slot * 128:(slot + 1) * 128], kpT[:], [bkpT], [])
                pb, bpb = bank()
                pv = pb[:].bitcast(BF16).rearrange("p (c n) -> p c n", c=8)
                for c in range(4):
                    tr(pv[:, c, :], ckvb[:, c * 128:(c + 1) * 128], ident[:], [bckvb, bid], [bpb])
                ckvT, bckvT = ckvT_r.next()
                cp(ckvT[:], pv[:, 0:4, :], [bpb], [bckvT], eng="act")
                return (ckvT, bckvT, KTd, VVd, slot)

            def kside1(ckvT, bckvT, KTd, VVd, slot):
                gb = []
                sm, bsm = sm_r.next()
                vb, bvb = vb_r.next()
                for g in range(4):
                    pb, bpb = bank()
                    for c in range(4):
                        mm(pb[:, :], ckvT[:, c, :], Wkv[:, c, 512 * g:512 * g + 512], c == 0, c == 3, [bckvT], [bpb])
                    gb.append((pb, bpb))
                    pvw = pb[:, :].rearrange("p (h n) -> p h n", h=2)
                    for hh in range(2):
                        act(junk[:, 128 * hh:128 * hh + 128], pvw[:, hh, 0:128], AF.Square, [bpb], [bsm],
                            accum_out=sm[:, 2 * g + hh:2 * g + hh + 1])
                    cp(vb[:, 2 * g:2 * g + 2, :], pvw[:, :, 128:256], [bpb], [bvb], eng="act")
                rstd(sm[:, 0:8], 128, [bsm], [bsm])
                knb, bknb = knb_r.next()
                for g in range(4):
                    pb, bpb = gb[g]
                    pvw = pb[:, :].rearrange("p (h n) -> p h n", h=2)
                    tt(knb[:, 2 * g:2 * g + 2, :], pvw[:, :, 0:128], bc(sm[:, 2 * g:2 * g + 2], [128, 2, 128], 2), ALU.mult,
                       [bpb, bsm], [bknb])
                dmas(VVd[:, slot, :, :].rearrange("h p n -> p h n"), vb[:], [bvb], [])
                return (knb, bknb, KTd, slot)

            def kside2(knb, bknb, KTd, slot):
                pb, bpb = bank()
                pv = pb[:].bitcast(BF16).rearrange("p (c n) -> p c n", c=8)
                for h in range(8):
                    tr(pv[:, h, :], knb[:, h, :], ident[:], [bknb, bid], [bpb])
                ktT, bktT = ktT_r.next()
                cp(ktT[:], pv, [bpb], [bktT], eng="act")
                dmas(KTd[:, :, slot * 128:(slot + 1) * 128].rearrange("h p n -> p h n"), ktT[:], [bktT], [])

            def kside(*a):
                kside2(*kside1(*kside0(*a)))

            def latent_part(hT, bhT, tb, btb, lat_out, kpe_out, KTd, KPEd, VVd, slot):
                pb, bpb = zgroup(hT, bhT, 768, 512)
                sm, bsm = sm_r.next()
                act(junk[:, 0:512], pb[:, :], AF.Square, [bpb], [bsm], accum_out=sm[:, 0:1])
                rstd(sm[:, 0:1], KVL, [bsm], [bsm])
                c32, bc32 = ckv32_r.next()
                stt(c32[:], pb[:, :], sm[:, 0:1], gkv_r[:], ALU.mult, ALU.mult, [bpb, bsm, bgkv], [bc32])
                if lat_out is not None:
                    dmas(lat_out, c32[:], [bc32], [])
                ckvb, bckvb = ckvb_r.next()
                cp(ckvb[:], c32[:], [bc32], [bckvb], eng="pool")
                pb, bpb = zgroup(hT, bhT, 1280, 64)
                act(junk[:, 0:64], pb[:, 0:64], AF.Square, [bpb], [bsm], accum_out=sm[:, 1:2])
                rstd(sm[:, 1:2], 64, [bsm], [bsm])
                kp, bkp = kp_r.next()
                stt(kp[:, 0, :], pb[:, 0:64], sm[:, 1:2], gkr_r[:], ALU.mult, ALU.mult, [bpb, bsm, bgkr], [bkp])
                tmp, btmp = tmp_r.next()
                rope(kp[:, 1:2, :], kp[:, 0:1, :], tb[:, 0:32], tb[:, 32:64], 1, 32,
                     tmp[:, 0:64].rearrange("p (h n) -> p h n", h=1), [bkp, btb, btmp], [bkp, btmp])
                if kpe_out is not None:
                    dmas(kpe_out, kp[:, 1, :], [bkp], [])
                kpb, bkpb = kpb_r.next()
                cp(kpb[:], kp[:, 1, :], [bkp], [bkpb], eng="pool")
                return (ckvb, bckvb, kpb, bkpb, KTd, KPEd, VVd, slot)

            def ret_kside(hT, bhT, tb, btb, want_rks, keep32=None):
                pb, bpb = zgroup(hT, bhT, 1856, 512)
                rkr, brkr = rkr_r.next()
                tmp, btmp = tmp_r.next()
                rope(rkr[:], pb[:, :].rearrange("p (h n) -> p h n", h=4), tb[:, 64:128], tb[:, 128:192], 4, 64,
                     tmp[:, 0:512].rearrange("p (h n) -> p h n", h=4), [bpb, btb, btmp], [brkr, btmp])
                kdec, bkdec = kdec_r.next()
                tt(kdec[:], rkr[:], bc(coef[:, C_KD:C_KD + 4], [128, 4, 128], 2), ALU.mult, [brkr, bcoef], [bkdec])
                rks = brks = None
                if want_rks:
                    rks, brks = rks_r.next()
                    ts(rks[:], rkr[:], 128 ** -0.5, None, ALU.mult, None, [brkr], [brks])
                if keep32 is not None:
                    ts(keep32[0][:], rkr[:], 128 ** -0.5, None, ALU.mult, None, [brkr], [keep32[1]])
                rvb, brvb = rvb_r.next()
                for g in range(2):
                    pv_, bpv_ = zgroup(hT, bhT, 2368 + 512 * g, 512)
                    cp(rvb[:, 2 * g:2 * g + 2, :], pv_[:, :].rearrange("p (h n) -> p h n", h=2), [bpv_], [brvb], eng="act")
                    if keep32 is not None:
                        cp(keep32[2][:, 2 * g:2 * g + 2, :], pv_[:, :].rearrange("p (h n) -> p h n", h=2), [bpv_], [keep32[3]], eng="dve")
                return kdec, bkdec, rvb, brvb, rks, brks

            def summary(kdec, bkdec, rvb, brvb):
                Ab = []
                for hp in range(2):
                    pa, bpa = bank()
                    for hh in range(2):
                        h = 2 * hp + hh
                        mm(pa[:, 256 * hh:256 * hh + 256], kdec[:, h, :], rvb[:, h, :], True, True, [bkdec, brvb], [bpa])
                    Ab.append((pa, bpa))
                return Ab

            def own_front(hT, bhT, tb, btb, u):
                pb, bpb = zgroup(hT, bhT, 1344, 512)
                rqr, brqr = rqr_r.next()
                tmp, btmp = tmp_r.next()
                rope(rqr[:], pb[:, :].rearrange("p (h n) -> p h n", h=4), tb[:, 64:128], tb[:, 128:192], 4, 64,
                     tmp[:, 0:512].rearrange("p (h n) -> p h n", h=4), [bpb, btb, btmp], [brqr, btmp])
                srg, bsrg = srg_r.next()
                for g in range(2):
                    pb, bpb = zgroup(hT, bhT, 3392 + 512 * g, 512)
                    act(srg[:, 512 * g:512 * g + 512], pb[:, :], AF.Silu, [bpb], [bsrg])
                return rqr, brqr, srg, bsrg

            def ret_post_a(src, bsrc, srg, bsrg, u):
                y32, by32 = y32_r.next()
                sm, bsm = sm_r.next()
                for h in range(4):
                    act(y32[:, h, :], src[h], AF.Identity, [bsrc[h]], [by32, bsm], accum_out=sm[:, h:h + 1])
                    act(junk[:, 256 * h:256 * h + 256], src[h], AF.Square, [bsrc[h]], [bsm], accum_out=sm[:, 4 + h:5 + h])
                ts(sm[:, 0:4], sm[:, 0:4], 1.0 / 256, None, ALU.mult, None, [bsm], [bsm])
                tt(sm[:, 8:12], sm[:, 0:4], sm[:, 0:4], ALU.mult, [bsm], [bsm])
                stt(sm[:, 4:8], sm[:, 4:8], 1.0 / 256, sm[:, 8:12], ALU.mult, ALU.subtract, [bsm], [bsm])
                act(sm[:, 4:8], sm[:, 4:8], AF.Sqrt, [bsm], [bsm], bias=EPS)
                recip(sm[:, 4:8], sm[:, 4:8], [bsm], [bsm])
                for h in range(4):
                    ts(y32[:, h, :], y32[:, h, :], sm[:, h:h + 1], sm[:, 4 + h:5 + h], ALU.subtract, ALU.mult, [by32, bsm], [by32])
                yf = y32[:].rearrange("p h n -> p (h n)")
                tt(yf, yf, gret_r[:], ALU.mult, [by32, bgret], [by32])
                retg, bretg = retg_r.next()
                tt(retg[:], yf, srg[:], ALU.mult, [by32, bsrg], [bretg])
                return (retg, bretg, u)

            def ret_post_b(retg, bretg, u):
                pb, bpb = bank()
                pv = pb[:].bitcast(BF16).rearrange("p (c n) -> p c n", c=8)
                for c in range(8):
                    tr(pv[:, c, :], retg[:, c * 128:(c + 1) * 128], ident[:], [bretg, bid], [bpb])
                retT, bretT = retT_r.next()
                cp(retT[:], pv, [bpb], [bretT], eng="act")
                dmas(RETT[:, :, u * 128:(u + 1) * 128].rearrange("c p n -> p c n"), retT[:], [bretT], [])

            def ret_post(src, bsrc, srg, bsrg, u):
                ret_post_b(*ret_post_a(src, bsrc, srg, bsrg, u))

            def ret_own(rqr, brqr, rks, brks, rvb, brvb, Sb, bSb, srg, bsrg, u):
                ret_post_b(*ro3(*ro2(*ro1(rqr, brqr, rks, brks)), rvb, brvb, Sb, bSb, srg, bsrg, u))

            def ro1(rqr, brqr, rks, brks):
                pq, bpq = bank()
                pqv = pq[:].bitcast(BF16).rearrange("p (c n) -> p c n", c=8)
                for h in range(4):
                    tr(pqv[:, h, :], rqr[:, h, :], ident[:], [brqr, bid], [bpq])
                    tr(pqv[:, 4 + h, :], rks[:, h, :], ident[:], [brks, bid], [bpq])
                rqT, brqT = rqT_r.next()
                rqdT, brqdT = rqdT_r.next()
                rkT, brkT = rkT_r.next()
                cp(rqT[:], pqv[:, 0:4, :], [bpq], [brqT], eng="act")
                tt(rqdT[:], pqv[:, 0:4, :], qdect[:], ALU.mult, [bpq, bqdec], [brqdT])
                cp(rkT[:], pqv[:, 4:8, :], [bpq], [brkT], eng="act")
                return (rqT, brqT, rqdT, brqdT, rkT, brkT)

            def ro2(rqT, brqT, rqdT, brqdT, rkT, brkT):
                ps_, bps_ = bank()
                psv = ps_[:, :].rearrange("p (h n) -> p h n", h=4)
                for h in range(4):
                    mm(psv[:, h, :], rkT[:, h, :], rqT[:, h, :], True, True, [brkT, brqT], [bps_])
                scm, bscm = scm_r.next()
                tt(scm[:], psv, DTt[:], ALU.mult, [bps_, bDT], [bscm])
                return (scm, bscm, rqdT, brqdT)

            def ro3(scm, bscm, rqdT, brqdT, rvb, brvb, Sb, bSb, srg, bsrg, u):
                src, bsrc = [], []
                for hp in range(2):
                    po, bpo = bank()
                    for hh in range(2):
                        h = 2 * hp + hh
                        o_ = po[:, 256 * hh:256 * hh + 256]
                        mm(o_, scm[:, h, :], rvb[:, h, :], True, False, [bscm, brvb], [bpo])
                        mm(o_, rqdT[:, h, :], Sb[:, h, :], False, True, [brqdT, bSb], [bpo])
                        src.append(o_)
                        bsrc.append(bpo)
                return ret_post_a(src, bsrc, srg, bsrg, u)

            def make_Sb(state, bstate):
                Sb, bSb = Sb_r.next()
                cp(Sb[:], state[:], [bstate], [bSb], eng="pool")
                return Sb, bSb

            def state_step(state, bstate, Ab):
                for h in range(4):
                    pa, bpa = Ab[h // 2]
                    stt(state[:, h, :], state[:, h, :], coef[:, C_G128 + h:C_G128 + h + 1], pa[:, 256 * (h % 2):256 * (h % 2) + 256],
                        ALU.mult, ALU.add, [bstate, bcoef, bpa], [bstate])

            def acc_add(accb, baccb, Ab, col):
                for h in range(4):
                    pa, bpa = Ab[h // 2]
                    stt(accb[:, h, :], pa[:, 256 * (h % 2):256 * (h % 2) + 256], coef[:, col + h:col + h + 1], accb[:, h, :],
                        ALU.mult, ALU.add, [baccb, bcoef, bpa], [baccb])

            tA_r = sb([128, 4, 256], F32, 1)

            def acc_add2(accb, baccb, Ab, col):
                tA, btA = tA_r.next()
                for h in range(4):
                    pa, bpa = Ab[h // 2]
                    act(tA[:, h, :], pa[:, 256 * (h % 2):256 * (h % 2) + 256], AF.Identity, [bpa, bcoef], [btA],
                        scale=coef[:, col + h:col + h + 1])
                tt(accb[:], accb[:], tA[:], ALU.add, [baccb, btA], [baccb], eng="pool")

            hT, bhT, tb, btb = front(xH2, tabH2)
            ret_kside(hT, bhT, tb, btb, False, keep32=(H2_rk32, bH2_rk32, H2_rv32, bH2_rv32))
            hT, bhT, tb, btb = front(xH, tabH)
            rqrH, brqrH, srgH, bsrgH = own_front(hT, bhT, tb, btb, UH)
            cp(H_rq32[:], rqrH[:], [brqrH], [bH_rq32])
            cp(H_srg[:], srgH[:], [bsrgH], [bH_srg], eng="pool")
            pq, bpq = bank()
            pqv = pq[:].bitcast(BF16).rearrange("p (c n) -> p c n", c=8)
            for h in range(4):
                tr(pqv[:, h, :], rqrH[:, h, :], ident[:], [brqrH, bid], [bpq])
            cp(H_rqT[:], pqv[:, 0:4, :], [bpq], [bH_rqT], eng="act")

            def phaseFa(o, p):
                T = 16 * o + p
                return front_a(xall[T], taball[T])

            def phaseZ(o, p, fr):
                hT, bhT, tb, btb = fr
                T = 16 * o + p
                own = p >= 12
                r = p - 12
                u = 4 * o + r
                kargs = latent_part(hT, bhT, tb, btb, latn[u] if own else None, kpen[u] if own else None, KT, KPE, VV, T)
                ow = own_front(hT, bhT, tb, btb, u) if own else None
                rk_ = ret_kside(hT, bhT, tb, btb, own)
                return (kargs, ow, rk_)

            def phaseY1(o, p, zz, k1args):
                kargs, ow, rk_ = zz
                own = p >= 12
                r = p - 12
                u = 4 * o + r
                if p == 0:
                    for h in range(4):
                        ts(acc_own[:, h, :], S_oct[:, h, :], coef[:, C_SO + h:C_SO + h + 1], None, ALU.mult, None, [bS_oct, bcoef], [bacc_own])
                        ts(acc_next[:, h, :], S_oct[:, h, :], coef[:, C_G2048 + h:C_G2048 + h + 1], None, ALU.mult, None, [bS_oct, bcoef], [bacc_next])
                r4 = None
                k2 = kside1(*k1args)
                kdec, bkdec, rvb, brvb, rks, brks = rk_
                Ab = summary(kdec, bkdec, rvb, brvb)
                if not own:
                    acc_add(acc_own, bacc_own, Ab, C_CO + 4 * p)
                    acc_add2(acc_next, bacc_next, Ab, C_CN + 4 * p)
                else:
                    rqr, brqr, srg, bsrg = ow
                    Sb, bSb = make_Sb(acc_own, bacc_own)
                    state_step(acc_own, bacc_own, Ab)
                    acc_add2(acc_next, bacc_next, Ab, C_CNO + 4 * r)
                    if r == 0:
                        for hp in range(2):
                            ph, bph = bank()
                            for hh in range(2):
                                h = 2 * hp + hh
                                mm(ph[:, 256 * hh:256 * hh + 256], H_rqT[:, h, :], Sb[:, h, :], True, True, [bH_rqT, bSb], [bph])
                            av = accH[:, 2 * hp:2 * hp + 2, :].rearrange("p h n -> p (h n)")
                            stt(av, ph[:, :], coef[:, C_SEL + o:C_SEL + o + 1], av, ALU.mult, ALU.add, [bph, bcoef, baccH], [baccH])
                    r4 = ro3(*ro2(*ro1(rqr, brqr, rks, brks)), rvb, brvb, Sb, bSb, srg, bsrg, u)
                if p == 15:
                    cp(S_oct[:], acc_next[:], [bacc_next], [bS_oct])
                return k2, r4

            seq = [(o, p) for o in range(NO) for p in range(16)]
            fa_q, fr_q, z_q, t_q, y_q = {}, {}, {}, {}, {}
            r4_pend = [None]
            fa_q[0] = phaseFa(*seq[0])
            fr_q[0] = front_b(*fa_q.pop(0))
            for n in range(1, len(seq) + 3):
                if n < len(seq):
                    fa_q[n] = phaseFa(*seq[n])
                if 0 <= n - 1 < len(seq):
                    z_q[n - 1] = phaseZ(*seq[n - 1], fr_q.pop(n - 1))
                if n < len(seq):
                    fr_q[n] = front_b(*fa_q.pop(n))
                if r4_pend[0] is not None:
                    ret_post_b(*r4_pend[0])
                    r4_pend[0] = None
                if 0 <= n - 2 < len(seq):
                    y_q[n - 2], r4n = phaseY1(*seq[n - 2], z_q.pop(n - 2), t_q.pop(n - 2))
                else:
                    r4n = None
                if 0 <= n - 1 < len(seq):
                    t_q[n - 1] = kside0(*z_q[n - 1][0])
                if 0 <= n - 3 < len(seq):
                    kside2(*y_q.pop(n - 3))
                r4_pend[0] = r4n
            if r4_pend[0] is not None:
                ret_post_b(*r4_pend[0])
            dmas(retp.rearrange("h d e -> d h e"), S_oct[:], [bS_oct], [])

            prod, bprod = tmp_r.next()
            pv3 = prod[:, 0:512].rearrange("p (h n) -> p h n", h=4)
            tt(pv3, H_rq32[:], H2_rk32[:], ALU.mult, [bH_rq32, bH2_rk32], [bprod])
            smh, bsmh = sm_r.next()
            red(smh[:, 0:4], pv3, [bprod], [bsmh])
            ts(smh[:, 0:4], smh[:, 0:4], coef[:, C_FLAG:C_FLAG + 1], None, ALU.mult, None, [bsmh, bcoef], [bsmh])
            for h in range(4):
                stt(accH[:, h, :], H2_rv32[:, h, :], smh[:, h:h + 1], accH[:, h, :], ALU.mult, ALU.subtract, [bH2_rv32, bsmh, baccH], [baccH])
                ts(accH[:, h, :], accH[:, h, :], coef[:, C_NINVG + h:C_NINVG + h + 1], None, ALU.mult, None, [baccH, bcoef], [baccH])
            ret_post([accH[:, h, :] for h in range(4)], [baccH] * 4, H_srg, bH_srg, UH)

            lat_r = sb([128, KVL], F32, 2)
            kpp_r = sb([128, 64], F32, 2)
            def p_load(t):
                lt, blt = lat_r.next()
                dma(lt[:], latp[t], [], [blt])
                kt_, bkt_ = kpp_r.next()
                dma(kt_[:], kpep[t], [], [bkt_])
                ckvb, bckvb = ckvb_r.next()
                cp(ckvb[:], lt[:], [blt], [bckvb], eng="pool")
                kpb, bkpb = kpb_r.next()
                cp(kpb[:], kt_[:], [bkt_], [bkpb], eng="pool")
                return (ckvb, bckvb, kpb, bkpb, KTs, KPEs, VVs, t)

            pl_q, p0_q, p1_q = {}, {}, {}
            for n in range(16 + 3):
                if n < 16:
                    pl_q[n] = p_load(n)
                if 0 <= n - 2 < 16:
                    p1_q[n - 2] = kside1(*p0_q.pop(n - 2))
                if 0 <= n - 1 < 16:
                    p0_q[n - 1] = kside0(*pl_q.pop(n - 1))
                if 0 <= n - 3 < 16:
                    kside2(*p1_q.pop(n - 3))

            dma(accS[:], S0.rearrange("h d e -> d h e"), [], [baccS])
            for h in range(4):
                ts(accS[:, h, :], accS[:, h, :], coef[:, C_CS + h:C_CS + h + 1], None, ALU.mult, None, [baccS, bcoef], [baccS])
            hT, bhT, tb, btb = front(xS, tabS)
            kargs = latent_part(hT, bhT, tb, btb, latS, kpeS, KTs, KPEs, VVs, 16)
            kside(*kargs)
            rqr, brqr, srg, bsrg = own_front(hT, bhT, tb, btb, US)
            kdec, bkdec, rvb, brvb, rks, brks = ret_kside(hT, bhT, tb, btb, True)
            Ab = summary(kdec, bkdec, rvb, brvb)
            Sb, bSb = make_Sb(accS, baccS)
            state_step(accS, baccS, Ab)
            ret_own(rqr, brqr, rks, brks, rvb, brvb, Sb, bSb, srg, bsrg, US)
            dmas(retS.rearrange("h d e -> d h e"), accS[:], [baccS], [])
            P.emit_stage()

        with contextlib.ExitStack() as st:
            cnt = [0]

            def sb(shape, dt, n=None):
                items = []
                for _ in range(n or 1):
                    cnt[0] += 1
                    items.append((st.enter_context(nc.sbuf_tensor("q%d" % cnt[0], list(shape), dt)), Buf()))
                return items[0] if n is None else Rot(items)

            banks = Rot([(st.enter_context(nc.psum_tensor("pq%d" % i, [128, 512], F32)), Buf()) for i in range(8)])
            WinQ, bWinQ = sb([128, 8, 768], BF16)
            WinG, bWinG = sb([128, 8, 2048], BF16)
            Wq, bWq = sb([128, 6, 1536], BF16)
            gcol, bgcol = sb([128, 24], F32)
            ident, bid = sb([128, 128], BF16)
            gqn_r, bgqn = sb([128, 128], F32)
            gqr_r, bgqr = sb([128, 64], F32)
            dma(ident[:], ident_d, [], [bid])
            dma(gqn_r[:], g_qn.partition_broadcast(128), [], [bgqn])
            dma(gqr_r[:], g_qr.partition_broadcast(128), [], [bgqr])
            gkn2, bgkn2 = sb([128, 128], F32)
            dma(gkn2[:], g_kn.partition_broadcast(128), [], [bgkn2])
            tt(gqn_r[:], gqn_r[:], gkn2[:], ALU.mult, [bgqn, bgkn2], [bgqn])
            P.dma(lambda: nc.sync.dma_start(out=gcol[:, 0:8], in_=g_mix.rearrange("o (c p) -> p (o c)", p=128),
                                            allow_slow_non_contiguous=True), [], [bgcol])
            P.dma(lambda: nc.sync.dma_start(out=gcol[:, 8:14], in_=g_ql.rearrange("o (c p) -> p (o c)", p=128),
                                            allow_slow_non_contiguous=True), [], [bgcol])
            with contextlib.ExitStack() as st2:
                stg = Rot([(st2.enter_context(nc.sbuf_tensor("qstg%d" % i, [128, 2048], F32)), Buf()) for i in range(10)])
                tog = [0]
                load_w(WinQ, bWinQ, w_in[:, 0:768], 8, 768, 0, [bgcol])
                load_w(WinG, bWinG, w_in[:, 4416:6464], 8, 2048, 0, [bgcol])
                load_w(Wq, bWq, w_q_up, 6, 1536, 8, [bgcol])
                P.emit_stage()
            xt_r = sb([128, D], F32, 2)
            tab_r = sb([128, 192], F32, 4)
            junk = st.enter_context(nc.sbuf_tensor("junkq", [128, D], BF16))
            sq_r = sb([128, 512], F32, 2)
            sm_r = sb([128, 64], F32, 4)
            xn_r = sb([128, D], BF16, 2)
            hT_r = sb([128, 8, 128], BF16, 2)
            tmp_r = sb([128, 1024], F32, 2)
            qln_r = sb([128, QL], BF16, 2)
            qlT_r = sb([128, 6, 128], BF16, 2)
            qn32_r = sb([128, 8, 128], F32, 2)
            qnb_r = sb([128, 8, 128], BF16, 2)
            qr32_r = sb([128, 8, 64], F32, 2)
            qrb_r = sb([128, 8, 64], BF16, 2)
            qT_r = sb([128, 8, 128], BF16, 2)
            qpT_r = sb([64, 8, 128], BF16, 2)
            gate_r = sb([128, 2048], BF16, 2)

            def bank():
                return banks.next()

            def zgroupQ(hT, bhT, c0, n):
                pb, bpb = bank()
                for c in range(8):
                    mm(pb[:, 0:n], hT[:, c, :], WinQ[:, c, c0:c0 + n], c == 0, c == 7, [bhT], [bpb])
                return pb, bpb

            def zgroupG(hT, bhT, c0, n):
                pb, bpb = bank()
                for c in range(8):
                    mm(pb[:, 0:n], hT[:, c, :], WinG[:, c, c0:c0 + n], c == 0, c == 7, [bhT], [bpb])
                return pb, bpb

            def front2(x_src, tab_src):
                xt, bxt = xt_r.next()
                dma(xt[:], x_src, [], [bxt])
                tb, btb = tab_r.next()
                dma(tb[:], tab_src, [], [btb])
                sm, bsm = sm_r.next()
                act(junk[:], xt[:], AF.Square, [bxt], [bsm], accum_out=sm[:, 0:1])
                rstd(sm[:, 0:1], D, [bsm], [bsm])
                xn, bxn = xn_r.next()
                ts(xn[:], xt[:], sm[:, 0:1], None, ALU.mult, None, [bxt, bsm], [bxn])
                pb, bpb = bank()
                pv = pb[:].bitcast(BF16).rearrange("p (c n) -> p c n", c=8)
                for c in range(8):
                    tr(pv[:, c, :], xn[:, c * 128:(c + 1) * 128], ident[:], [bxn, bid], [bpb])
                hT, bhT = hT_r.next()
                cp(hT[:], pv, [bpb], [bhT], eng="act")
                return hT, bhT, tb, btb

            def q1(hT, bhT):
                pa, bpa = zgroupQ(hT, bhT, 0, 512)
                pb2, bpb2 = zgroupQ(hT, bhT, 512, 256)
                sm, bsm = sm_r.next()
                act(junk[:, 0:512], pa[:, :], AF.Square, [bpa], [bsm], accum_out=sm[:, 0:1])
                act(junk[:, 512:768], pb2[:, 0:256], AF.Square, [bpb2], [bsm], accum_out=sm[:, 1:2])
                tt(sm[:, 0:1], sm[:, 0:1], sm[:, 1:2], ALU.add, [bsm], [bsm])
                rstd(sm[:, 0:1], QL, [bsm], [bsm])
                qln, bqln = qln_r.next()
                ts(qln[:, 0:512], pa[:, :], sm[:, 0:1], None, ALU.mult, None, [bpa, bsm], [bqln])
                ts(qln[:, 512:768], pb2[:, 0:256], sm[:, 0:1], None, ALU.mult, None, [bpb2, bsm], [bqln])
                return qln, bqln

            def gates(hT, bhT, u):
                gt, bgt = gate_r.next()
                for g in range(4):
                    pb, bpb = zgroupG(hT, bhT, 512 * g, 512)
                    act(gt[:, 512 * g:512 * g + 512], pb[:, :], AF.Sigmoid, [bpb], [bgt])
                dmas(GATE[u], gt[:], [bgt], [])

            def q2a(qln, bqln):
                pb, bpb = bank()
                pv = pb[:].bitcast(BF16).rearrange("p (c n) -> p c n", c=8)
                for c in range(6):
                    tr(pv[:, c, :], qln[:, c * 128:(c + 1) * 128], ident[:], [bqln, bid], [bpb])
                qlT, bqlT = qlT_r.next()
                cp(qlT[:], pv[:, 0:6, :], [bpb], [bqlT], eng="act")
                return qlT, bqlT

            def q2b(qlT, bqlT):
                gb = []
                sm2, bsm2 = sm_r.next()
                for g in range(4):
                    pb, bpb = bank()
                    for c in range(6):
                        mm(pb[:, 0:384], qlT[:, c, :], Wq[:, c, 384 * g:384 * g + 384], c == 0, c == 5, [bqlT], [bpb])
                    gb.append((pb, bpb))
                    pvw = pb[:, 0:384].rearrange("p (h n) -> p h n", h=2)
                    for hh in range(2):
                        act(junk[:, 192 * hh:192 * hh + 128], pvw[:, hh, 0:128], AF.Square, [bpb], [bsm2],
                            accum_out=sm2[:, 2 * g + hh:2 * g + hh + 1])
                        act(junk[:, 512 + 64 * hh:512 + 64 * hh + 64], pvw[:, hh, 128:192], AF.Square, [bpb], [bsm2],
                            accum_out=sm2[:, 8 + 2 * g + hh:8 + 2 * g + hh + 1])
                rstd(sm2[:, 0:8], 128, [bsm2], [bsm2])
                rstd(sm2[:, 8:16], 64, [bsm2], [bsm2])
                qn32, bqn32 = qn32_r.next()
                qr32, bqr32 = qr32_r.next()
                for g in range(4):
                    pb, bpb = gb[g]
                    pvw = pb[:, 0:384].rearrange("p (h n) -> p h n", h=2)
                    tt(qn32[:, 2 * g:2 * g + 2, :], pvw[:, :, 0:128], bc(sm2[:, 2 * g:2 * g + 2], [128, 2, 128], 2), ALU.mult,
                       [bpb, bsm2], [bqn32])
                    tt(qr32[:, 2 * g:2 * g + 2, :], pvw[:, :, 128:192], bc(sm2[:, 8 + 2 * g:8 + 2 * g + 2], [128, 2, 64], 2), ALU.mult,
                       [bpb, bsm2], [bqr32])
                return qn32, bqn32, qr32, bqr32

            def q3(qn32, bqn32, qr32, bqr32, tb, btb, u):
                qnb, bqnb = qnb_r.next()
                tt(qnb[:], qn32[:], bc(gqn_r[:], [128, 8, 128], 1), ALU.mult, [bqn32, bgqn], [bqnb])
                tt(qr32[:], qr32[:], bc(gqr_r[:], [128, 8, 64], 1), ALU.mult, [bqr32, bgqr], [bqr32])
                qrb, bqrb = qrb_r.next()
                tmp, btmp = tmp_r.next()
                rope(qrb[:], qr32[:], tb[:, 0:32], tb[:, 32:64], 8, 32,
                     tmp[:, 0:512].rearrange("p (h n) -> p h n", h=8), [bqr32, btb, btmp], [bqrb, btmp])
                pb, bpb = bank()
                pv = pb[:].bitcast(BF16).rearrange("p (c n) -> p c n", c=8)
                for h in range(8):
                    tr(pv[:, h, :], qnb[:, h, :], ident[:], [bqnb, bid], [bpb])
                qT, bqT = qT_r.next()
                cp(qT[:], pv, [bpb], [bqT], eng="act")
                dmas(QT[:, :, u * 128:(u + 1) * 128].rearrange("h p n -> p h n"), qT[:], [bqT], [])
                pb, bpb = bank()
                pv = pb[:].bitcast(BF16).rearrange("p (c n) -> p c n", c=8)
                for h in range(8):
                    tr(pv[0:64, h, :], qrb[:, h, :], ident[:], [bqrb, bid], [bpb])
                qpT, bqpT = qpT_r.next()
                cp(qpT[:], pv[0:64, :, :], [bpb], [bqpT], eng="act")
                dmas(QPE[:, :, u * 128:(u + 1) * 128].rearrange("h p n -> p h n"), qpT[:], [bqpT], [])

            def front2a(x_src, tab_src):
                xt, bxt = xt_r.next()
                dma(xt[:], x_src, [], [bxt])
                tb, btb = tab_r.next()
                dma(tb[:], tab_src, [], [btb])
                sm, bsm = sm_r.next()
                act(junk[:], xt[:], AF.Square, [bxt], [bsm], accum_out=sm[:, 0:1])
                rstd(sm[:, 0:1], D, [bsm], [bsm])
                xn, bxn = xn_r.next()
                ts(xn[:], xt[:], sm[:, 0:1], None, ALU.mult, None, [bxt, bsm], [bxn])
                return xn, bxn, tb, btb

            def front2b(xn, bxn, tb, btb):
                pb, bpb = bank()
                pv = pb[:].bitcast(BF16).rearrange("p (c n) -> p c n", c=8)
                for c in range(8):
                    tr(pv[:, c, :], xn[:, c * 128:(c + 1) * 128], ident[:], [bxn, bid], [bpb])
                hT, bhT = hT_r.next()
                cp(hT[:], pv, [bpb], [bhT], eng="act")
                return hT, bhT, tb, btb

            def usrc(u):
                if u == UH:
                    return xH, tabH
                if u == US:
                    return xS, tabS
                o_, r_ = divmod(u, 4)
                return xall[16 * o_ + 12 + r_], taball[16 * o_ + 12 + r_]

            fr = {0: front2b(*front2a(*usrc(0)))}
            fa = {}
            q3_pend = None
            for u in range(NU):
                hT, bhT, tb, btb = fr.pop(u)
                if u + 1 < NU:
                    fa[u + 1] = front2a(*usrc(u + 1))
                qln, bqln = q1(hT, bhT)
                gates(hT, bhT, u)
                qlT, bqlT = q2a(qln, bqln)
                if u + 1 < NU:
                    fr[u + 1] = front2b(*fa.pop(u + 1))
                if q3_pend is not None:
                    q3(*q3_pend)
                qq = q2b(qlT, bqlT)
                q3_pend = qq + (tb, btb, u)
            q3(*q3_pend)
            P.emit_stage()

        with contextlib.ExitStack() as st:
            cnt = [0]

            def sb(shape, dt, n=None):
                items = []
                for _ in range(n or 1):
                    cnt[0] += 1
                    items.append((st.enter_context(nc.sbuf_tensor("b%d" % cnt[0], list(shape), dt)), Buf()))
                return items[0] if n is None else Rot(items)

            sbanks = Rot([(st.enter_context(nc.psum_tensor("pbs%d" % i, [128, 512], F32)), Buf()) for i in range(5)])
            obanks = Rot([(st.enter_context(nc.psum_tensor("pbo%d" % i, [128, 512], F32)), Buf()) for i in range(2)])
            prbank = (st.enter_context(nc.psum_tensor("pbr", [128, 512], F32)), Buf())
            coef, bcoef = sb([128, 256], F32)
            cmask, bcm = sb([128, 128], BF16)
            maskH, bmH = sb([128, NT, 8], BF16)
            ones32, bones = sb([128, 128], F32)
            tiny, btiny = sb([128, 1], F32)
            P.op("dve", lambda: nc.vector.memset(tiny[:], 1e-30), [], [btiny])
            acc_r = sb([128, 512], F32, 2)
            kpeT, bkpeT = sb([128, (NT // 2) * 128], BF16)
            kpeTs, bkpeTs = sb([128, 9 * 128], BF16)
            dma(coef[:], coef_d, [], [bcoef])
            dma(cmask[:], cmask_d, [], [bcm])
            dma(maskH[:], maskH_d.rearrange("p (t n) -> p t n", n=8), [], [bmH])
            P.op("dve", lambda: nc.vector.memset(ones32[:], 1.0), [], [bones])
            kv_ = KPE[:, 0:NT * 128].rearrange("p (m two n) -> p m two n", two=2, n=128)
            dma(kpeT[0:64, :].rearrange("p (m n) -> p m n", n=128), kv_[:, :, 0, :], [], [bkpeT])
            dma(kpeT[64:128, :].rearrange("p (m n) -> p m n", n=128), kv_[:, :, 1, :], [], [bkpeT])
            P.op("dve", lambda: nc.vector.memset(kpeTs[:], 0.0), [], [bkpeTs])
            ks_ = KPEs[:, 0:16 * 128].rearrange("p (m two n) -> p m two n", two=2, n=128)
            dma(kpeTs[0:64, 0:8 * 128].rearrange("p (m n) -> p m n", n=128), ks_[:, :, 0, :], [], [bkpeTs])
            dma(kpeTs[64:128, 0:8 * 128].rearrange("p (m n) -> p m n", n=128), ks_[:, :, 1, :], [], [bkpeTs])
            dma(kpeTs[0:64, 8 * 128:9 * 128], KPEs[:, 16 * 128:17 * 128], [], [bkpeTs])
            kT_r = sb([128, NT * 128], BF16, 2)
            v_r = sb([128, NT, 128], BF16, 2)
            kTs_r = sb([128, 17 * 128], BF16, 2)
            vs_r = sb([128, 17, 128], BF16, 2)
            q_r = sb([128, NQ], BF16, 2)
            qp_r = sb([128, NQ], BF16, 2)
            pT_r = sb([128, 512], BF16, 6)
            ri_r = sb([128, 512], F32, 2)
            at_r = sb([128, NQ], BF16, 2)

            epi_pend = [None]
            for h in range(NH):
                kT, bkT = kT_r.next(); dma(kT[:], KT[h, :, 0:NT * 128], [], [bkT])
                vv, bvv = v_r.next(); dma(vv[:], VV[h, 0:NT].rearrange("t p n -> p t n"), [], [bvv])
                kTs, bkTs = kTs_r.next(); dma(kTs[:], KTs[h], [], [bkTs])
                vs, bvs = vs_r.next(); dma(vs[:], VVs[h].rearrange("t p n -> p t n"), [], [bvs])
                q, bq = q_r.next(); dma(q[:], QT[h], [], [bq])
                qp, bqp = qp_r.next(); dma(qp[0:64, :], QPE[h], [], [bqp]); dma(qp[64:128, :], QPE[h], [], [bqp])
                at, bat = at_r.next()

                def group(units, qc0, ncols):
                    po, bpo = obanks.next()
                    pr, bpr = prbank
                    acc, bacc = acc_r.next()
                    pend = []
                    nu = len(units)

                    def pv_stage(i, pT, bpT, c_off):
                        n = ncols - c_off
                        u_ = units[i]
                        mm(po[:, c_off:ncols], u_[2], pT[:, 0:n], i == 0, i == nu - 1, [bpT] + u_[6], [bpo])
                        if i == 0:
                            cp(acc[:, 0:ncols], pT[:, 0:n], [bpT], [bacc])
                        else:
                            tt(acc[:, c_off:ncols], acc[:, c_off:ncols], pT[:, 0:n], ALU.add, [bpT, bacc], [bacc])

                    def qk_nope(i):
                        ka, kpa, va, c_off, bias, msk, RB = units[i]
                        n = ncols - c_off
                        ps_, bps_ = sbanks.next()
                        mm(ps_[:, 0:n], ka, q[:, qc0 + c_off:qc0 + ncols], True, False, RB + [bq], [bps_])
                        return ps_, bps_

                    def qk_rope(i, ps_, bps_):
                        ka, (kpa, hf_), va, c_off, bias, msk, RB = units[i]
                        n = ncols - c_off
                        mm(ps_[:, 0:n], kpa, qp[64 * hf_:64 * hf_ + 64, qc0 + c_off:qc0 + ncols], False, True, RB + [bqp], [bps_])

                    def soft(i, ps_, bps_):
                        ka, kpa, va, c_off, bias, msk, RB = units[i]
                        n = ncols - c_off
                        pT, bpT = pT_r.next()
                        if bias is None:
                            act(pT[:, 0:n], ps_[:, 0:n], AF.Exp, [bps_], [bpT], scale=MLA_SCALE)
                        else:
                            act(pT[:, 0:n], ps_[:, 0:n], AF.Exp, [bps_, bcoef], [bpT], scale=MLA_SCALE, bias=bias)
                        if msk is not None:
                            tt(pT[:, msk[1]:msk[2]], pT[:, msk[1]:msk[2]], msk[0], ALU.mult, [bpT] + msk[3], [bpT])
                        pend.append((i, pT, bpT, c_off))

                    for i0 in range(0, nu, 2):
                        ii = [i for i in (i0, i0 + 1) if i < nu]
                        sbs = [qk_nope(i) for i in ii]
                        for i, sb_ in zip(ii, sbs):
                            qk_rope(i, *sb_)
                        for i, sb_ in zip(ii, sbs):
                            soft(i, *sb_)
                        if i0 == 4 and epi_pend[0] is not None:
                            epi_pend[0]()
                            epi_pend[0] = None
                        while len(pend) > 3:
                            pv_stage(*pend.pop(0))
                    while pend:
                        pv_stage(*pend.pop(0))
                    if epi_pend[0] is not None:
                        epi_pend[0]()
                        epi_pend[0] = None

                    def epilogue(po=po, bpo=bpo, acc=acc, bacc=bacc, at=at, bat=bat, qc0=qc0, ncols=ncols):
                        mm(pr[:, 0:ncols], ones32[:], acc[:, 0:ncols], True, True, [bacc, bones], [bpr])
                        ri, bri = ri_r.next()
                        act(ri[:, 0:ncols], pr[:, 0:ncols], AF.Ln, [bpr, btiny], [bri], bias=tiny[:, 0:1])
                        act(ri[:, 0:ncols], ri[:, 0:ncols], AF.Exp, [bri], [bri], scale=-1.0)
                        tt(at[:, qc0:qc0 + ncols], po[:, 0:ncols], ri[:, 0:ncols], ALU.mult, [bpo, bri], [bat])

                    epi_pend[0] = epilogue

                def ktile(T):
                    hf_ = T % 2
                    return (kT[:, T * 128:(T + 1) * 128], (kpeT[64 * hf_:64 * hf_ + 64, (T // 2) * 128:(T // 2 + 1) * 128], hf_), vv[:, T, :])

                for o in range(NO):
                    units = []
                    for o2 in range(o + 1):
                        for p in range(16):
                            T = 16 * o2 + p
                            ka, kpa, va = ktile(T)
                            RB = [bkT, bkpeT, bvv]
                            if o2 < o:
                                units.append((ka, kpa, va, 0, None, None, RB))
                            elif p < 12:
                                units.append((ka, kpa, va, 0, coef[:, C_VB + p:C_VB + p + 1], None, RB))
                            else:
                                r = p - 12
                                units.append((ka, kpa, va, 128 * r, None, (cmask[:], 0, 128, [bcm]), RB))
                    group(units, 512 * o, 512)
                units = []
                for T in range(NT):
                    if T % 16 >= 12:
                        continue
                    ka, kpa, va = ktile(T)
                    units.append((ka, kpa, va, 0, None, (maskH[:, T, 0:2 * NO], 0, 2 * NO, [bmH]), [bkT, bkpeT, bvv]))
                group(units, 128 * UH, 2 * NO)
                units = []
                for T in range(17):
                    units.append((kTs[:, T * 128:(T + 1) * 128], (kpeTs[64 * (T % 2):64 * (T % 2) + 64, (T // 2) * 128:(T // 2 + 1) * 128], T % 2), vs[:, T, :], 0,
                                  coef[:, C_VBS:C_VBS + 1] if T == 16 else None, None, [bkTs, bkpeTs, bvs]))
                group(units, 128 * US, 128)
                if epi_pend[0] is not None:
                    epi_pend[0]()
                    epi_pend[0] = None
                P.op("pool", lambda at=at: nc.gpsimd.memset(at[:, 128 * UH + 2 * NO:128 * UH + 128], 0.0), [], [bat])
                dmas(ATT[h], at[:], [bat], [])
            P.emit_stage()

        with contextlib.ExitStack() as st:
            cnt = [0]

            def sb(shape, dt, n=None):
                items = []
                for _ in range(n or 1):
                    cnt[0] += 1
                    items.append((st.enter_context(nc.sbuf_tensor("c%d" % cnt[0], list(shape), dt)), Buf()))
                return items[0] if n is None else Rot(items)

            banks = Rot([(st.enter_context(nc.psum_tensor("pc%d" % i, [128, 512], F32)), Buf()) for i in range(8)])
            WbA, bWbA = sb([128, 8, D], BF16)
            WbB, bWbB = sb([128, 8, D], BF16)
            Wo, bWo = sb([128, 8, D], BF16)
            stg = sb([128, 1024], F32, 10)
            ident, bid = sb([128, 128], BF16)
            dma(ident[:], ident_d, [], [bid])
            tog = [0]

            def load_w2(dst, bdst, src, KC, N, stg, gcolt=None, bg=None):
                for c in range(KC):
                    for n0 in range(0, N, 1024):
                        n1 = min(N, n0 + 1024)
                        s_, bs_ = stg.next()
                        dma(s_[:, 0:n1 - n0], src[c * 128:(c + 1) * 128, n0:n1], [], [bs_])
                        tog[0] += 1
                        if gcolt is None:
                            cp(dst[:, c, n0:n1], s_[:, 0:n1 - n0], [bs_], [], eng="act" if tog[0] % 2 else "dve")
                        elif tog[0] % 2:
                            act(dst[:, c, n0:n1], s_[:, 0:n1 - n0], AF.Identity, [bs_, bg], [], scale=gcolt[:, c:c + 1])
                        else:
                            ts(dst[:, c, n0:n1], s_[:, 0:n1 - n0], gcolt[:, c:c + 1], None, ALU.mult, None, [bs_, bg], [])
                cp(stg.items[0][0][:, 0:1], stg.items[0][0][:, 0:1], [], [it[1] for it in stg.items] + [bdst], eng="dve")

            load_w2(WbA, bWbA, w_ob[0:1024, :], 8, D, stg)
            load_w2(WbB, bWbB, w_ob[1024:2048, :], 8, D, stg)
            load_w2(Wo, bWo, w_out, 8, D, stg)
            xt_r = sb([128, D], F32, 3)
            gt_r = sb([128, 2048], BF16, 2)
            aT_r = sb([128, 8, 128], BF16, 2)
            rT_r = sb([128, 8, 128], BF16, 2)
            t32_r = sb([128, D], F32, 2)
            mg_r = sb([128, D], BF16, 2)
            mT_r = sb([128, 8, 128], BF16, 2)
            xm_r = sb([128, D], F32, 2)
            sq_r = sb([128, D], F32, 1)
            sm_r = sb([128, 8], F32, 2)
            xn_r = sb([128, D], BF16, 2)
            hn_r = sb([128, 8, 128], BF16, 2)

            def xsrc(u):
                if u == UH:
                    return xH
                if u == US:
                    return xS
                o, r = divmod(u, 4)
                return xall[16 * o + 12 + r]

            def c1_p1(u):
                xt, bxt = xt_r.next(); dma(xt[:], xsrc(u), [], [bxt])
                gt, bgt = gt_r.next(); dma(gt[:], GATE[u], [], [bgt])
                aT, baT = aT_r.next(); dma(aT[:], ATT[:, :, u * 128:(u + 1) * 128].rearrange("h p n -> p h n"), [], [baT])
                rT, brT = rT_r.next(); dma(rT[:], RETT[:, :, u * 128:(u + 1) * 128].rearrange("c p n -> p c n"), [], [brT])
                t32, bt32 = t32_r.next()
                mg, bmg = mg_r.next()
                for g in range(2):
                    pa, bpa = banks.next()
                    for c in range(8):
                        mm(pa[:, :], aT[:, c, :], WbA[:, c, 512 * g:512 * g + 512], c == 0, c == 7, [baT, bWbA], [bpa])
                    pr_, bpr_ = banks.next()
                    for c in range(8):
                        mm(pr_[:, :], rT[:, c, :], WbB[:, c, 512 * g:512 * g + 512], c == 0, c == 7, [brT, bWbB], [bpr_])
                    sl = slice(512 * g, 512 * g + 512)
                    tt(t32[:, sl], pa[:, :], gt[:, 512 * g:512 * g + 512], ALU.mult, [bpa, bgt], [bt32])
                    tt(mg[:, sl], pr_[:, :], gt[:, 1024 + 512 * g:1024 + 512 * g + 512], ALU.mult, [bpr_, bgt], [bmg])
                tt(mg[:], mg[:], t32[:], ALU.add, [bmg, bt32], [bmg])
                return (u, xt, bxt, mg, bmg)

            def c1_p2a(u, xt, bxt, mg, bmg):
                pb, bpb = banks.next()
                pv = pb[:].bitcast(BF16).rearrange("p (c n) -> p c n", c=8)
                for c in range(8):
                    tr(pv[:, c, :], mg[:, c * 128:(c + 1) * 128], ident[:], [bmg, bid], [bpb])
                mT, bmT = mT_r.next()
                cp(mT[:], pv, [bpb], [bmT], eng="act")
                return (u, xt, bxt, mT, bmT)

            def c1_p2b(u, xt, bxt, mT, bmT):
                xm, bxm = xm_r.next()
                for g in range(2):
                    po_, bpo_ = banks.next()
                    for c in range(8):
                        mm(po_[:, :], mT[:, c, :], Wo[:, c, 512 * g:512 * g + 512], c == 0, c == 7, [bmT, bWo], [bpo_])
                    tt(xm[:, 512 * g:512 * g + 512], po_[:, :], xt[:, 512 * g:512 * g + 512], ALU.add, [bpo_, bxt], [bxm])
                dmas(XMID[u], xm[:], [bxm], [])
                sq, bsq = sq_r.next()
                sm, bsm = sm_r.next()
                act(sq[:], xm[:], AF.Square, [bxm], [bsq, bsm], accum_out=sm[:, 0:1])
                rstd(sm[:, 0:1], D, [bsm], [bsm])
                xn, bxn = xn_r.next()
                ts(xn[:], xm[:], sm[:, 0:1], None, ALU.mult, None, [bxm, bsm], [bxn])
                return (u, xn, bxn)

            def c1_p3(u, xn, bxn):
                pb, bpb = banks.next()
                pv = pb[:].bitcast(BF16).rearrange("p (c n) -> p c n", c=8)
                for c in range(8):
                    tr(pv[:, c, :], xn[:, c * 128:(c + 1) * 128], ident[:], [bxn, bid], [bpb])
                hn, bhn = hn_r.next()
                cp(hn[:], pv, [bpb], [bhn], eng="act")
                dmas(HNT[:, :, u * 128:(u + 1) * 128].rearrange("c p n -> p c n"), hn[:], [bhn], [])

            p1q, p3q = {}, {}
            for n in range(NU + 2):
                if n < NU:
                    p1q[n] = c1_p1(n)
                a2 = None
                if 0 <= n - 1 < NU:
                    a2 = c1_p2a(*p1q.pop(n - 1))
                if 0 <= n - 2 < NU:
                    c1_p3(*p3q.pop(n - 2))
                if a2 is not None:
                    p3q[n - 1] = c1_p2b(*a2)
            P.emit_stage()

        with contextlib.ExitStack() as st:
            cnt = [0]

            def sbw(shape, dt):
                cnt[0] += 1
                return st.enter_context(nc.sbuf_tensor("w%d" % cnt[0], list(shape), dt)), Buf()

            Wup, bWup = sbw([128, 8, 2 * DFF], BF16)
            Wdn, bWdn = sbw([128, 22, D], BF16)
            cwT, bcwT = sbw([128, 4, NCH], F32)
            uH, buH = sbw([128, NCH, 8], F32)
            c0T, bc0T = sbw([128, NCH, 2], F32)
            with contextlib.ExitStack() as st2:
                stg = Rot([(st2.enter_context(nc.sbuf_tensor("wstg%d" % i, [128, 1024], F32)), Buf()) for i in range(10)])
                gcol2 = st2.enter_context(nc.sbuf_tensor("gcol2", [128, 8], F32)); bg2 = Buf()
                cwl = st2.enter_context(nc.sbuf_tensor("cwl", [NCH, 4, 128], F32)); bcwl = Buf()
                c0l = st2.enter_context(nc.sbuf_tensor("c0l", [NCH, 2, 128], F32)); bc0l = Buf()
                id32 = st2.enter_context(nc.sbuf_tensor("id32", [128, 128], F32)); bid32 = Buf()
                pw = st2.enter_context(nc.psum_tensor("pw", [128, 512], F32)); bpw = Buf()
                P.dma(lambda: nc.sync.dma_start(out=gcol2[:], in_=g_ffn.rearrange("o (c p) -> p (o c)", p=128),
                                                allow_slow_non_contiguous=True), [], [bg2])
                dma(id32[:], ident32_d, [], [bid32])
                dma(cwl[:, 0:3, :], cw.rearrange("k (c p) -> c k p", p=128), [], [bcwl])
                dma(cwl[:, 3:4, :], cb.rearrange("k (c p) -> c k p", p=128), [], [bcwl])
                dma(c0l[:], conv0.rearrange("k (c p) -> c k p", p=128), [], [bc0l])
                for k in range(4):
                    tr(pw[:, k * NCH:(k + 1) * NCH], cwl[:, k, :], id32[0:NCH, 0:NCH], [bcwl, bid32], [bpw])
                for k in range(2):
                    tr(pw[:, (4 + k) * NCH:(5 + k) * NCH], c0l[:, k, :], id32[0:NCH, 0:NCH], [bc0l, bid32], [bpw])
                cp(cwT[:], pw[:, 0:4 * NCH].rearrange("p (k c) -> p k c", k=4), [bpw], [bcwT])
                cp(c0T[:], pw[:, 4 * NCH:6 * NCH].rearrange("p (k c) -> p c k", k=2), [bpw], [bc0T])
                tog = [0]
                load_w2(Wup, bWup, w_up, 8, 2 * DFF, stg, gcol2, bg2)
                load_w2(Wdn, bWdn, w_dn, 22, D, stg)
                P.emit_stage()

            def sb(shape, dt, n=None):
                items = []
                for _ in range(n or 1):
                    cnt[0] += 1
                    items.append((st.enter_context(nc.sbuf_tensor("d%d" % cnt[0], list(shape), dt)), Buf()))
                return items[0] if n is None else Rot(items)

            banks = Rot([(st.enter_context(nc.psum_tensor("pd%d" % i, [128, 512], F32)), Buf()) for i in range(8)])
            coef, bcoef = sb([128, 256], F32)
            dma(coef[:], coef_d, [], [bcoef])
            hn_r = sb([128, 8, 512], BF16, 1)
            aT_r = sb([128, 22, 512], BF16, 1)
            ue_r = sb([128, 514], F32, 4)
            tm_r = sb([128, 512], F32, 6)
            xm_r = sb([128, D], F32, 2)
            y_r = sb([128, D], F32, 2)
            cst, bcst = sb([128, NCH, 2], F32)
            bW = []

            def ffn_group(col0, N, tiles, halo, cst_out, hc=0):
                hn, bhn = hn_r.next()
                dma(hn[:, :, 0:N], HNT[:, :, col0:col0 + N].rearrange("c p n -> p c n"), [], [bhn])
                aT, baT = aT_r.next()
                pend_s = [None]

                def finish_pair(tms_, i_):
                    (tg, btg), (tv, btv) = tms_
                    act(tg[:, 0:N], tg[:, 0:N], AF.Silu, [btg], [btg])
                    tt(aT[:, i_, 0:N], tg[:, 0:N], tv[:, 0:N], ALU.mult, [btg, btv], [baT])

                for i in range(22):
                    tms = []
                    ues = []
                    for k, ch in enumerate((i, 22 + i)):
                        ue, bue = ue_r.next()
                        tm, btm = tm_r.next()
                        pb, bpb = banks.next()
                        for c in range(8):
                            mm(pb[:, 0:N], Wup[:, c, ch * 128:(ch + 1) * 128], hn[:, c, 0:N], c == 0, c == 7, [bhn], [bpb])
                        if tiles is not None:
                            act(tm[:, 0:N], pb[:, 0:N], AF.Identity, [bpb], [btm], scale=cwT[:, 2, ch:ch + 1], bias=cwT[:, 3, ch:ch + 1])
                        cp(ue[:, 2:N + 2], pb[:, 0:N], [bpb], [bue], eng="act")
                        if halo is not None:
                            cp(ue[:, hc:hc + 2], halo(ch), [buH, bc0T], [bue], eng="act")
                        if cst_out is not None:
                            cp(cst[:, ch, :], ue[:, N:N + 2], [bue], [bcst], eng="act")
                        ues.append((ue, bue))
                        if tiles is None:
                            continue
                        stt(tm[:, 0:N], ue[:, 1:N + 1], cwT[:, 1, ch:ch + 1], tm[:, 0:N], ALU.mult, ALU.add, [bue, btm], [btm])
                        stt(tm[:, 0:N], ue[:, 0:N], cwT[:, 0, ch:ch + 1], tm[:, 0:N], ALU.mult, ALU.add, [bue, btm], [btm])
                        tms.append((tm, btm))
                    if tiles is None:
                        for k, ch in enumerate((i, 22 + i)):
                            ue, bue = ues[k]
                            tt(uH[:, ch, :], ue[:, 2:10], coef[:, C_HF:C_HF + 8], ALU.mult, [bue, bcoef], [buH])
                        continue
                    if pend_s[0] is not None:
                        finish_pair(*pend_s[0])
                    pend_s[0] = (tms, i)
                if pend_s[0] is not None:
                    finish_pair(*pend_s[0])
                    pend_s[0] = None
                if tiles is None:
                    return
                if cst_out is not None:
                    dmas(cst_out.rearrange("c p k -> p c k"), cst[:], [bcst], [])
                for t, (u, out_ap) in enumerate(tiles):
                    xm, bxm = xm_r.next()
                    dma(xm[:], XMID[u], [], [bxm])
                    y, by = y_r.next()
                    for g in range(2):
                        pb, bpb = banks.next()
                        for i in range(22):
                            mm(pb[:, :], aT[:, i, 128 * t:128 * t + 128], Wdn[:, i, 512 * g:512 * g + 512], i == 0, i == 21, [baT], [bpb])
                        tt(y[:, 512 * g:512 * g + 512], pb[:, :], xm[:, 512 * g:512 * g + 512], ALU.add, [bpb, bxm], [by])
                    dmas(out_ap, y[:], [by], [])

            P.op("dve", lambda: nc.vector.memset(uH[:], 0.0), [], [buH])
            ffn_group(128 * UH, 128, None, None, None)
            for o in range(NO):
                ffn_group(512 * o, 512, [(4 * o + r, yp[4 * o + r]) for r in range(4)],
                          (lambda ch, o=o: uH[:, ch, 2 * o:2 * o + 2]), convp if o == NO - 1 else None)
            ffn_group(128 * US, 128, [(US, ys)], (lambda ch: c0T[:, ch, :]), convS, hc=112)
            P.emit_stage()
    return nc


def _tables(pos):
    pos = np.asarray(pos, np.float32)[:, None]
    fm = (THETA ** (-np.arange(32, dtype=np.float32) / 32)).astype(np.float32)[None, :]
    fr = (THETA ** (-np.arange(64, dtype=np.float32) / 64)).astype(np.float32)[None, :]
    am = (pos * fm).astype(np.float32)
    ar = (pos * fr).astype(np.float32)
    return np.concatenate([np.cos(am), np.sin(am), np.cos(ar), np.sin(ar)], axis=1).astype(np.float32)


def _gammas():
    return (1.0 - 2.0 ** (-5.0 - np.arange(RH, dtype=np.float64)))


def _consts():
    gam = _gammas()
    idx = np.arange(128)
    DT = np.zeros((128, 4, 128), np.float32)
    qdec = np.zeros((128, 4, 128), np.float32)
    for h in range(4):
        d = idx[None, :] - idx[:, None]
        DT[:, h, :] = np.where(d >= 0, gam[h] ** np.maximum(d, 0), 0.0)
        qdec[:, h, :] = (gam[h] ** (idx + 1.0))[None, :]
    cmask = (idx[:, None] // 64 <= idx[None, :] // 64).astype(np.float32)
    return DT.reshape(128, 512), qdec.reshape(128, 512), cmask


def _coef(j, NO):
    gam = _gammas()
    c = np.zeros((128, 256), np.float64)
    for h in range(4):
        g = gam[h]
        for p in range(12):
            orig = p if p < 4 * j else p + 4
            c[:, 0 + 4 * p + h] = g ** (128.0 * (4 * j - 1 - p)) if p < 4 * j else 0.0
            c[:, 48 + 4 * p + h] = g ** (128.0 * (15 - orig))
        for r in range(4):
            c[:, 96 + 4 * r + h] = g ** (128.0 * (15 - 4 * j - r))
        c[:, 112 + h] = g ** (512.0 * j)
        c[:, 116 + h] = g ** 2048.0
        c[:, 120 + h] = g ** 128.0
        c[:, 124 + h] = g ** (-112.0)
        c[:, 128 + h] = (g ** (127.0 - np.arange(128))) * (128 ** -0.5)
        c[:, 132 + h] = np.where(np.arange(128) % 2 == 0, -1.0 / g, -1.0)
    c[:, 136] = (np.arange(128) % 2 == 0).astype(np.float64)
    for o in range(NO):
        c[2 * o:2 * o + 2, 140 + o] = 1.0
    for o in range(NO):
        c[:, 148 + 2 * o:148 + 2 * o + 2] = 0.0 if (o == 0 and j == 0) else 1.0
    for p in range(12):
        c[:, 160 + p] = 0.0 if p < 4 * j else NEG
    c[:, 172] = np.where(np.arange(128) < 112, NEG, 0.0)
    return c.astype(np.float32)


def _maskH(j, NO):
    NT = 16 * NO
    m = np.zeros((128, NT, 8), np.float32)
    for o in range(NO):
        for o2 in range(NO):
            for p in range(16):
                vis = (o2 < o) or (o2 == o and p < 12 and p < 4 * j)
                if vis:
                    m[:, 16 * o2 + p, 2 * o:2 * o + 2] = 1.0
    return m.reshape(128, NT * 8)


def host_prep(inp, NO, core):
    bf = ml_dtypes.bfloat16
    b, j = divmod(core, 4)
    xp = np.asarray(inp["x_prompt"][b], np.float32)
    xt = xp.reshape(NO, 16, 128, D)
    order = [p if p < 4 * j else p + 4 for p in range(12)] + [4 * j + r for r in range(4)]
    xall = np.ascontiguousarray(xt[:, order]).reshape(16 * NO, 128, D)
    pos = np.arange(NO * 2048).reshape(NO, 16, 128)[:, order].reshape(16 * NO, 128)
    taball = np.stack([_tables(pos[t]) for t in range(16 * NO)])
    xH = np.zeros((128, D), np.float32); xH2 = np.zeros((128, D), np.float32)
    pH = np.zeros(128); pH2 = np.zeros(128)
    for o in range(NO):
        s = 2048 * o + 512 * j
        if s >= 2:
            xH[2 * o] = xp[s - 2]; xH[2 * o + 1] = xp[s - 1]
            xH2[2 * o] = xp[s - 1]; xH2[2 * o + 1] = xp[s - 1]
            pH[2 * o] = s - 2; pH[2 * o + 1] = s - 1
            pH2[2 * o] = s - 1; pH2[2 * o + 1] = s - 1
    xS = np.zeros((128, D), np.float32)
    xS[112:] = np.asarray(inp["x_sample"][core], np.float32)
    pS = np.zeros(128); pS[112:] = PAST + np.arange(16)
    DT, qdec, cmask = _consts()
    m = {
        "xall": xall, "taball": taball, "xH": xH, "tabH": _tables(pH), "xH2": xH2, "tabH2": _tables(pH2),
        "xS": xS, "tabS": _tables(pS),
        "latp": np.ascontiguousarray(inp["cache_mla_latent"][0, core], np.float32).reshape(16, 128, KVL),
        "kpep": np.ascontiguousarray(inp["cache_mla_rope_key"][0, core], np.float32).reshape(16, 128, 64),
        "S0": np.ascontiguousarray(inp["state_retention"][0, core], np.float32),
        "conv0": np.ascontiguousarray(inp["state_ffn_conv"][0, core], np.float32),
        "ident": np.eye(128).astype(bf), "ident32": np.eye(128, dtype=np.float32),
        "cmask": cmask.astype(bf), "DT": DT, "qdec": qdec,
        "coef": _coef(j, NO), "maskH": _maskH(j, NO).astype(bf),
    }
    return m


def shared_inputs(inp):
    f = lambda k: np.ascontiguousarray(np.asarray(inp[k], np.float32)[0])
    return {
        "w_in": f("w_in"), "w_q_up": f("w_q_up"), "w_kv_up": f("w_kv_up"), "w_ob": f("w_o_branch"), "w_out": f("w_out"),
        "w_up": f("w_ffn_up"), "w_dn": f("w_ffn_down"), "cw": f("ffn_conv_w"), "cb": f("ffn_conv_b").reshape(1, -1),
        "g_mix": f("g_norm_mix").reshape(1, -1), "g_ql": f("g_q_lat").reshape(1, -1), "g_qn": f("g_q_nope").reshape(1, -1),
        "g_qr": f("g_q_rope").reshape(1, -1), "g_kv": f("g_kv_lat").reshape(1, -1), "g_kn": f("g_k_nope").reshape(1, -1),
        "g_kr": f("g_k_rope").reshape(1, -1), "g_ret": f("g_ret_out").reshape(1, -1), "g_ffn": f("g_norm_ffn").reshape(1, -1),
    }


def assemble(results, NO, nb):
    SEQ = 2048 * NO
    ncore = len(results)
    y_p = np.zeros((nb, SEQ, D), np.float32)
    lat_p = np.zeros((1, nb, SEQ, KVL), np.float32)
    kpe_p = np.zeros((1, nb, SEQ, 64), np.float32)
    ret_p = np.zeros((1, nb, RH, 128, 256), np.float32)
    conv_p = np.zeros((1, nb, 2, 2 * DFF), np.float32)
    for c in range(4 * nb):
        b, j = divmod(c, 4)
        r = results[c]
        for o in range(NO):
            s = 2048 * o + 512 * j
            y_p[b, s:s + 512] = r["yp"][4 * o:4 * o + 4].reshape(512, D)
            lat_p[0, b, s:s + 512] = r["latn"][4 * o:4 * o + 4].reshape(512, KVL)
            kpe_p[0, b, s:s + 512] = r["kpen"][4 * o:4 * o + 4].reshape(512, 64)
        if j == 3:
            ret_p[0, b] = r["retp"]
            conv_p[0, b] = r["convp"].reshape(NCH * 128, 2).T
    y_s = np.stack([results[c]["ys"][112:] for c in range(ncore)])
    lat_s = np.stack([results[c]["latS"][112:] for c in range(ncore)])[None]
    kpe_s = np.stack([results[c]["kpeS"][112:] for c in range(ncore)])[None]
    ret_s = np.stack([results[c]["retS"] for c in range(ncore)])[None]
    conv_s = np.stack([results[c]["convS"].reshape(NCH * 128, 2).T for c in range(ncore)])[None]
    return (y_p, y_s, lat_p, kpe_p, ret_p, conv_p, lat_s, kpe_s, ret_s, conv_s)


def kernel(**inputs):
    NO = 4
    nc = build(NO)
    sh = shared_inputs(inputs)
    in_maps = []
    for c in range(8):
        m = host_prep(inputs, NO, c)
        m.update(sh)
        in_maps.append(m)
    res = run_bass_kernel_spmd(nc, in_maps, core_ids=list(range(8)))
    return assemble(res.results, NO, 2)
```

```python
import contextlib
import numpy as np
import ml_dtypes
import concourse.bass as bass
import concourse.mybir as mybir
from concourse.bass_utils import run_bass_kernel_spmd

F32 = mybir.dt.float32
BF16 = mybir.dt.bfloat16
ALU = mybir.AluOpType
AF = mybir.ActivationFunctionType
AX = mybir.AxisListType

D = 1024
EPS = 1e-6
THETA = 10000.0
NH = 8
QL = 768
KVL = 512
RH = 4
DFF = 2816
NCH = 44
PAST = 2048
MLA_SCALE = 192 ** -0.5
NEG = -30000.0
COMPUTE = ("pe", "act", "dve", "pool")


class Buf:
    __slots__ = ("writers", "readers")

    def __init__(self):
        self.writers = []
        self.readers = []


class Op:
    __slots__ = ("eng", "fn", "deps", "dma", "pos", "idx", "sig", "semval", "dsem", "dval", "guard", "epoch")


class Prog:
    def __init__(self, nc, st, nds=48):
        self.nc = nc
        self.nds = nds
        self.esem = {e: st.enter_context(nc.semaphore("s_" + e)) for e in COMPUTE}
        self.dsem = [st.enter_context(nc.semaphore("d_%d" % i)) for i in range(nds)]
        self.bar = st.enter_context(nc.semaphore("s_bar"))
        self.ecnt = {e: 0 for e in COMPUTE}
        self.last_use = [None] * nds
        self.use_cnt = [0] * nds
        self.k = 0
        self.k2 = 0
        self.epoch = 0
        self.nbar = 0
        self.gidx = 0
        self._reset()

    def _reset(self):
        self.ops = []
        self.streams = {e: [] for e in ("pe", "act", "dve", "pool", "sp")}

    def _add(self, eng, fn, reads, writes, dma):
        op = Op()
        op.eng, op.fn, op.dma = eng, fn, dma
        op.sig, op.semval, op.dsem, op.dval, op.guard = False, 0, None, 0, None
        op.epoch = self.epoch
        op.idx = self.gidx
        self.gidx += 1
        op.pos = len(self.streams[eng])
        deps = []
        for b in reads:
            deps.extend(b.writers)
        for b in writes:
            deps.extend(b.writers)
            deps.extend(b.readers)
        seen = set()
        op.deps = []
        for d in deps:
            if d.epoch != self.epoch or d.idx in seen:
                continue
            seen.add(d.idx)
            op.deps.append(d)
        for b in writes:
            b.writers = [op]
            b.readers = []
        for b in reads:
            b.readers.append(op)
        self.ops.append(op)
        self.streams[eng].append(op)
        return op

    def op(self, eng, fn, reads=(), writes=()):
        return self._add(eng, fn, reads, writes, False)

    def dma(self, fn, reads=(), writes=(), eng="sp"):
        return self._add(eng, fn, reads, writes, True)

    def emit_stage(self):
        nc = self.nc
        waited = {e: {f: -1 for f in COMPUTE} for e in self.streams}
        waited_dma = {e: set() for e in self.streams}
        waits = {}
        for op in self.ops:
            e = op.eng
            wl = []
            for d in op.deps:
                if d.dma:
                    if d.idx in waited_dma[e]:
                        continue
                    waited_dma[e].add(d.idx)
                    wl.append(d)
                else:
                    f = d.eng
                    if f == e:
                        if e == "pe":
                            continue
                    if waited[e][f] >= d.pos:
                        continue
                    waited[e][f] = d.pos
                    wl.append(d)
            waits[op.idx] = wl
            for d in wl:
                d.sig = True
        for e in COMPUTE:
            cops = [o for o in self.streams[e] if not o.dma]
            if cops:
                cops[-1].sig = True
            for op in self.streams[e]:
                if not op.dma and op.sig:
                    self.ecnt[e] += 1
                    op.semval = self.ecnt[e]
        for op in self.ops:
            if op.dma:
                if op.eng == "sp":
                    s = self.k % (self.nds - 16)
                    self.k += 1
                else:
                    s = self.nds - 16 + (self.k2 % 16)
                    self.k2 += 1
                op.dsem = s
                op.guard = self.last_use[s]
                self.use_cnt[s] += 1
                op.dval = 16 * self.use_cnt[s]
                self.last_use[s] = op
        self.nbar += 1
        nbar = self.nbar
        esem, dsem, bar = self.esem, self.dsem, self.bar
        finals = dict(self.ecnt)
        dfinals = [16 * c for c in self.use_cnt]

        def wait_on(eo, d):
            if d.dma:
                eo.wait_ge(dsem[d.dsem], d.dval)
            else:
                eo.wait_ge(esem[d.eng], d.semval)

        streams = self.streams
        with nc.Block() as blk:
            deco = {"pe": blk.tensor, "act": blk.scalar, "dve": blk.vector, "pool": blk.gpsimd, "sp": blk.sync}

            def make(e, stream):
                def body(eo):
                    for op in stream:
                        for d in waits[op.idx]:
                            wait_on(eo, d)
                        if op.dma and op.guard is not None:
                            eo.wait_ge(dsem[op.guard.dsem], op.guard.dval)
                        ins = op.fn()
                        if op.dma:
                            ins.then_inc(dsem[op.dsem], 16)
                        elif op.sig:
                            ins.then_inc(esem[e], 1)
                    for f in COMPUTE:
                        if f != e and finals[f] > 0:
                            eo.wait_ge(esem[f], finals[f])
                    if e == "sp":
                        for i in range(self.nds):
                            if dfinals[i] > 0:
                                eo.wait_ge(dsem[i], dfinals[i])
                        eo.sem_inc(bar, 1)
                    else:
                        eo.wait_ge(bar, nbar)
                return body

            for e in ("sp", "pe", "act", "dve", "pool"):
                deco[e](make(e, streams[e]))
        self.epoch += 1
        self._reset()


class Rot:
    def __init__(self, items):
        self.items = items
        self.i = 0

    def next(self):
        it = self.items[self.i % len(self.items)]
        self.i += 1
        return it


def bc(ap, shape, axis):
    return ap.unsqueeze(axis).to_broadcast(shape)


def build(NO, debug=False):
    NT = 16 * NO
    NOWN = 4 * NO
    UH = NOWN
    US = NOWN + 1
    NU = NOWN + 2
    NQ = 128 * NU
    NSL = NT + 1
    nc = bass.Bass("TRN2", target_bir_lowering=False)

    def din(name, shape, dt=F32):
        return nc.dram_tensor(name, list(shape), dt, kind="ExternalInput").ap()

    def dout(name, shape, dt=F32):
        return nc.dram_tensor(name, list(shape), dt, kind="ExternalOutput").ap()

    def dscr(name, shape, dt=BF16):
        return nc.dram_tensor(name, list(shape), dt, kind="ExternalOutput" if debug else "Internal").ap()

    xall = din("xall", [NT, 128, D])
    taball = din("taball", [NT, 128, 192])
    xH = din("xH", [128, D]); tabH = din("tabH", [128, 192])
    xH2 = din("xH2", [128, D]); tabH2 = din("tabH2", [128, 192])
    xS = din("xS", [128, D]); tabS = din("tabS", [128, 192])
    latp = din("latp", [16, 128, KVL]); kpep = din("kpep", [16, 128, 64])
    S0 = din("S0", [RH, 128, 256]); conv0 = din("conv0", [2, 2 * DFF])
    w_in = din("w_in", [D, 6464]); w_q_up = din("w_q_up", [QL, 1536]); w_kv_up = din("w_kv_up", [KVL, 2048])
    w_ob = din("w_ob", [2048, D]); w_out = din("w_out", [D, D])
    w_up = din("w_up", [D, 2 * DFF]); w_dn = din("w_dn", [DFF, D])
    cw = din("cw", [3, 2 * DFF]); cb = din("cb", [1, 2 * DFF])
    g_mix = din("g_mix", [1, D]); g_ql = din("g_ql", [1, QL]); g_qn = din("g_qn", [1, 128]); g_qr = din("g_qr", [1, 64])
    g_kv = din("g_kv", [1, KVL]); g_kn = din("g_kn", [1, 128]); g_kr = din("g_kr", [1, 64])
    g_ret = din("g_ret", [1, 1024]); g_ffn = din("g_ffn", [1, D])
    ident_d = din("ident", [128, 128], BF16); ident32_d = din("ident32", [128, 128])
    cmask_d = din("cmask", [128, 128], BF16)
    DT_d = din("DT", [128, 4 * 128]); qdec_d = din("qdec", [128, 4 * 128])
    coef_d = din("coef", [128, 256])
    maskH_d = din("maskH", [128, NT * 8], BF16)

    yp = dout("yp", [NOWN, 128, D]); ys = dout("ys", [128, D])
    latn = dout("latn", [NOWN, 128, KVL]); kpen = dout("kpen", [NOWN, 128, 64])
    retp = dout("retp", [RH, 128, 256]); convp = dout("convp", [NCH, 128, 2])
    latS = dout("latS", [128, KVL]); kpeS = dout("kpeS", [128, 64])
    retS = dout("retS", [RH, 128, 256]); convS = dout("convS", [NCH, 128, 2])

    KT = dscr("KT", [NH, 128, NSL * 128]); KPE = dscr("KPE", [64, NSL * 128]); VV = dscr("VV", [NH, NSL, 128, 128])
    KTs = dscr("KTs", [NH, 128, 17 * 128]); KPEs = dscr("KPEs", [64, 17 * 128]); VVs = dscr("VVs", [NH, 17, 128, 128])
    QT = dscr("QT", [NH, 128, NQ]); QPE = dscr("QPE", [NH, 64, NQ])
    RETT = dscr("RETT", [8, 128, NQ]); GATE = dscr("GATE", [NU, 128, 2048])
    ATT = dscr("ATT", [NH, 128, NQ])
    XMID = dscr("XMID", [NU, 128, D], F32); HNT = dscr("HNT", [8, 128, NQ])

    C_CO, C_CN, C_CNO, C_SO, C_G2048, C_G128, C_CS = 0, 48, 96, 112, 116, 120, 124
    C_KD, C_NINVG, C_FLAG, C_SEL, C_HF, C_VB, C_VBS = 128, 132, 136, 140, 148, 160, 172

    with contextlib.ExitStack() as top:
        P = Prog(nc, top)

        def mm(out, lhsT, rhs, start, stop, R, W):
            return P.op("pe", lambda: nc.tensor.matmul(out, lhsT=lhsT, rhs=rhs, start=start, stop=stop), R, W)

        def tr(out, in_, idn, R, W):
            return P.op("pe", lambda: nc.tensor.transpose(out, in_, idn), R, W)

        def act(out, in_, func, R, W, **kw):
            return P.op("act", lambda: nc.scalar.activation(out=out, in_=in_, func=func, **kw), R, W)

        def tt(out, in0, in1, op, R, W, eng="dve"):
            e = nc.vector if eng == "dve" else nc.gpsimd
            return P.op(eng, lambda: e.tensor_tensor(out=out, in0=in0, in1=in1, op=op), R, W)

        def ts(out, in0, s1, s2, op0, op1, R, W, eng="dve"):
            e = nc.vector if eng == "dve" else nc.gpsimd
            if s2 is None:
                return P.op(eng, lambda: e.tensor_scalar(out=out, in0=in0, scalar1=s1, scalar2=None, op0=op0), R, W)
            return P.op(eng, lambda: e.tensor_scalar(out=out, in0=in0, scalar1=s1, scalar2=s2, op0=op0, op1=op1), R, W)

        def stt(out, in0, sc, in1, op0, op1, R, W, eng="dve"):
            e = nc.vector if eng == "dve" else nc.gpsimd
            return P.op(eng, lambda: e.scalar_tensor_tensor(out=out, in0=in0, scalar=sc, in1=in1, op0=op0, op1=op1), R, W)

        def red(out, in_, R, W):
            return P.op("dve", lambda: nc.vector.tensor_reduce(out=out, in_=in_, axis=AX.X, op=ALU.add), R, W)

        def cp(out, in_, R, W, eng="dve"):
            if eng == "act":
                return P.op("act", lambda: nc.scalar.copy(out=out, in_=in_), R, W)
            e = nc.vector if eng == "dve" else nc.gpsimd
            return P.op(eng, lambda: e.tensor_copy(out=out, in_=in_), R, W)

        def recip(out, in_, R, W):
            return P.op("dve", lambda: nc.vector.reciprocal(out=out, in_=in_), R, W)

        def dma(out, in_, R, W):
            return P.dma(lambda: nc.sync.dma_start(out=out, in_=in_), R, W)

        def dmas(out, in_, R, W):
            return P.dma(lambda: nc.gpsimd.dma_start(out=out, in_=in_), R, W, eng="pool")

        def rstd(ss_ap, n, R, W):
            act(ss_ap, ss_ap, AF.Sqrt, R, W, scale=1.0 / n, bias=EPS)
            recip(ss_ap, ss_ap, R, W)

        def rope(dst, src, cos, sin, H, half, tmp, R, W):
            shp = [128, H, half]
            c = bc(cos, shp, 1)
            s = bc(sin, shp, 1)
            x1, x2 = src[:, :, 0:half], src[:, :, half:2 * half]
            t1, t2 = tmp[:, :, 0:half], tmp[:, :, half:2 * half]
            tt(t1, x1, c, ALU.mult, R, W)
            tt(t2, x2, s, ALU.mult, R, W)
            tt(dst[:, :, 0:half], t1, t2, ALU.subtract, R, W)
            tt(t1, x1, s, ALU.mult, R, W)
            tt(t2, x2, c, ALU.mult, R, W)
            tt(dst[:, :, half:2 * half], t1, t2, ALU.add, R, W)

        with contextlib.ExitStack() as st:
            cnt = [0]

            def sb(shape, dt, n=None):
                items = []
                for _ in range(n or 1):
                    cnt[0] += 1
                    items.append((st.enter_context(nc.sbuf_tensor("a%d" % cnt[0], list(shape), dt)), Buf()))
                return items[0] if n is None else Rot(items)

            banks = Rot([(st.enter_context(nc.psum_tensor("pa%d" % i, [128, 512], F32)), Buf()) for i in range(8)])

            Win, bWin = sb([128, 8, 3648], BF16)
            Wkv, bWkv = sb([128, 4, 2048], BF16)
            gcol, bgcol = sb([128, 24], F32)
            ident, bid = sb([128, 128], BF16)
            coef, bcoef = sb([128, 256], F32)
            DTt, bDT = sb([128, 4, 128], F32)
            qdect, bqdec = sb([128, 4, 128], F32)
            gkv_r, bgkv = sb([128, KVL], F32)
            gkn_r, bgkn = sb([128, 128], F32)
            gkr_r, bgkr = sb([128, 64], F32)
            gret_r, bgret = sb([128, 1024], F32)

            dma(ident[:], ident_d, [], [bid])
            dma(coef[:], coef_d, [], [bcoef])
            dma(DTt[:], DT_d.rearrange("p (h n) -> p h n", h=4), [], [bDT])
            dma(qdect[:], qdec_d.rearrange("p (h n) -> p h n", h=4), [], [bqdec])
            dma(gkv_r[:], g_kv.partition_broadcast(128), [], [bgkv])
            dma(gkn_r[:], g_kn.partition_broadcast(128), [], [bgkn])
            dma(gkr_r[:], g_kr.partition_broadcast(128), [], [bgkr])
            dma(gret_r[:], g_ret.partition_broadcast(128), [], [bgret])
            P.dma(lambda: nc.sync.dma_start(out=gcol[:, 0:8], in_=g_mix.rearrange("o (c p) -> p (o c)", p=128),
                                            allow_slow_non_contiguous=True), [], [bgcol])
            P.dma(lambda: nc.sync.dma_start(out=gcol[:, 8:14], in_=g_ql.rearrange("o (c p) -> p (o c)", p=128),
                                            allow_slow_non_contiguous=True), [], [bgcol])

            tog = [0]

            def load_w(dst, bdst, src, KC, N, gc0, gR):
                for c in range(KC):
                    for n0 in range(0, N, 2048):
                        n1 = min(N, n0 + 2048)
                        s_, bs_ = stg.next()
                        dma(s_[:, 0:n1 - n0], src[c * 128:(c + 1) * 128, n0:n1], [], [bs_])
                        tog[0] += 1
                        if gc0 is None:
                            cp(dst[:, c, n0:n1], s_[:, 0:n1 - n0], [bs_], [], eng="act" if tog[0] % 2 else "dve")
                        elif tog[0] % 2:
                            act(dst[:, c, n0:n1], s_[:, 0:n1 - n0], AF.Identity, [bs_] + gR, [], scale=gcol[:, gc0 + c:gc0 + c + 1])
                        else:
                            ts(dst[:, c, n0:n1], s_[:, 0:n1 - n0], gcol[:, gc0 + c:gc0 + c + 1], None, ALU.mult, None, [bs_] + gR, [])
                cp(stg.items[0][0][:, 0:1], stg.items[0][0][:, 0:1], [], [it[1] for it in stg.items] + [bdst], eng="dve")

            with contextlib.ExitStack() as st2:
                stg = Rot([(st2.enter_context(nc.sbuf_tensor("astg%d" % i, [128, 2048], F32)), Buf()) for i in range(10)])
                load_w(Win, bWin, w_in[:, 768:4416], 8, 3648, 0, [bgcol])
                load_w(Wkv, bWkv, w_kv_up, 4, 2048, None, [])
                P.emit_stage()

            xt_r = sb([128, D], F32, 2)
            tab_r = sb([128, 192], F32, 2)
            junk = st.enter_context(nc.sbuf_tensor("junk", [128, D], BF16))
            sq_r = sb([128, 512], F32, 2)
            sm_r = sb([128, 64], F32, 4)
            xn_r = sb([128, D], BF16, 2)
            hT_r = sb([128, 8, 128], BF16, 2)
            ckv32_r = sb([128, KVL], F32, 1)
            ckvb_r = sb([128, KVL], BF16, 2)
            ckvT_r = sb([128, 4, 128], BF16, 2)
            kp_r = sb([128, 2, 64], F32, 2)
            tmp_r = sb([128, 1024], F32, 1)
            kpb_r = sb([128, 64], BF16, 2)
            kpT_r = sb([64, 128], BF16, 2)
            vb_r = sb([128, 8, 128], BF16, 1)
            knb_r = sb([128, 8, 128], BF16, 2)
            ktT_r = sb([128, 8, 128], BF16, 1)
            rkr_r = sb([128, 4, 128], F32, 1)
            kdec_r = sb([128, 4, 128], BF16, 2)
            rks_r = sb([128, 4, 128], BF16, 2)
            rvb_r = sb([128, 4, 256], BF16, 2)
            rqr_r = sb([128, 4, 128], BF16, 2)
            srg_r = sb([128, 1024], F32, 2)
            rqT_r = sb([128, 4, 128], BF16, 1)
            rqdT_r = sb([128, 4, 128], BF16, 1)
            rkT_r = sb([128, 4, 128], BF16, 1)
            scm_r = sb([128, 4, 128], BF16, 1)
            y32_r = sb([128, 4, 256], F32, 1)
            retg_r = sb([128, 1024], BF16, 1)
            retT_r = sb([128, 8, 128], BF16, 1)
            S_oct, bS_oct = sb([128, 4, 256], F32)
            acc_own, bacc_own = sb([128, 4, 256], F32)
            acc_next, bacc_next = sb([128, 4, 256], F32)
            accS, baccS = sb([128, 4, 256], F32)
            Sb_r = sb([128, 4, 256], BF16, 2)
            accH, baccH = sb([128, 4, 256], F32)
            H_rq32, bH_rq32 = sb([128, 4, 128], F32)
            H_rqT, bH_rqT = sb([128, 4, 128], BF16)
            H2_rk32, bH2_rk32 = sb([128, 4, 128], F32)
            H2_rv32, bH2_rv32 = sb([128, 4, 256], F32)
            H_srg, bH_srg = sb([128, 1024], F32)

            P.op("dve", lambda: nc.vector.memset(S_oct[:], 0.0), [], [bS_oct])
            P.op("dve", lambda: nc.vector.memset(accH[:], 0.0), [], [baccH])

            def bank():
                t, b = banks.next()
                return t, b

            def front_a(x_src, tab_src):
                xt, bxt = xt_r.next()
                dma(xt[:], x_src, [], [bxt])
                tb, btb = tab_r.next()
                dma(tb[:], tab_src, [], [btb])
                sm, bsm = sm_r.next()
                act(junk[:], xt[:], AF.Square, [bxt], [bsm], accum_out=sm[:, 0:1])
                rstd(sm[:, 0:1], D, [bsm], [bsm])
                xn, bxn = xn_r.next()
                ts(xn[:], xt[:], sm[:, 0:1], None, ALU.mult, None, [bxt, bsm], [bxn])
                return xn, bxn, tb, btb

            def front_b(xn, bxn, tb, btb):
                pb, bpb = bank()
                pv = pb[:].bitcast(BF16).rearrange("p (c n) -> p c n", c=8)
                for c in range(8):
                    tr(pv[:, c, :], xn[:, c * 128:(c + 1) * 128], ident[:], [bxn, bid], [bpb])
                hT, bhT = hT_r.next()
                cp(hT[:], pv, [bpb], [bhT], eng="act")
                return hT, bhT, tb, btb

            def front(x_src, tab_src):
                return front_b(*front_a(x_src, tab_src))

            def zgroup(hT, bhT, c0, n):
                pb, bpb = bank()
                for c in range(8):
                    mm(pb[:, 0:n], hT[:, c, :], Win[:, c, c0 - 768:c0 - 768 + n], c == 0, c == 7, [bhT], [bpb])
                return pb, bpb

            def kside0(ckvb, bckvb, kpb, bkpb, KTd, KPEd, VVd, slot):
                pb, bpb = bank()
                pv = pb[:].bitcast(BF16)
                tr(pv[0:64, 0:128], kpb[:], ident[:], [bkpb, bid], [bpb])
                kpT, bkpT = kpT_r.next()
                cp(kpT[:], pv[0:64, 0:128], [bpb], [bkpT], eng="act")
                dmas(KPEd[:, slot * 128:(slot + 1) * 128], kpT[:], [bkpT], [])
                pb, bpb = bank()
                pv = pb[:].bitcast(BF16).rearrange("p (c n) -> p c n", c=8)
                for c in range(4):
                    tr(pv[:, c, :], ckvb[:, c * 128:(c + 1) * 128], ident[:], [bckvb, bid], [bpb])
                ckvT, bckvT = ckvT_r.next()
                cp(ckvT[:], pv[:, 0:4, :], [bpb], [bckvT], eng="act")
                return (ckvT, bckvT, KTd, VVd, slot)

            def kside1(ckvT, bckvT, KTd, VVd, slot):
                gb = []
                sm, bsm = sm_r.next()
                vb, bvb = vb_r.next()
                for g in range(4):
                    pb, bpb = bank()
                    for c in range(4):
                        mm(pb[:, :], ckvT[:, c, :], Wkv[:, c, 512 * g:512 * g + 512], c == 0, c == 3, [bckvT], [bpb])
                    gb.append((pb, bpb))
                    pvw = pb[:, :].rearrange("p (h n) -> p h n", h=2)
                    for hh in range(2):
                        act(junk[:, 128 * hh:128 * hh + 128], pvw[:, hh, 0:128], AF.Square, [bpb], [bsm],
                            accum_out=sm[:, 2 * g + hh:2 * g + hh + 1])
                    cp(vb[:, 2 * g:2 * g + 2, :], pvw[:, :, 128:256], [bpb], [bvb], eng="act")
                rstd(sm[:, 0:8], 128, [bsm], [bsm])
                knb, bknb = knb_r.next()
                for g in range(4):
                    pb, bpb = gb[g]
                    pvw = pb[:, :].rearrange("p (h n) -> p h n", h=2)
                    tt(knb[:, 2 * g:2 * g + 2, :], pvw[:, :, 0:128], bc(sm[:, 2 * g:2 * g + 2], [128, 2, 128], 2), ALU.mult,
                       [bpb, bsm], [bknb])
                dmas(VVd[:, slot, :, :].rearrange("h p n -> p h n"), vb[:], [bvb], [])
                return (knb, bknb, KTd, slot)

            def kside2(knb, bknb, KTd, slot):
                pb, bpb = bank()
                pv = pb[:].bitcast(BF16).rearrange("p (c n) -> p c n", c=8)
                for h in range(8):
                    tr(pv[:, h, :], knb[:, h, :], ident[:], [bknb, bid], [bpb])
                ktT, bktT = ktT_r.next()
                cp(ktT[:], pv, [bpb], [bktT], eng="act")
                dmas(KTd[:, :, slot * 128:(slot + 1) * 128].rearrange("h p n -> p h n"), ktT[:], [bktT], [])

            def kside(*a):
                kside2(*kside1(*kside0(*a)))

            def latent_part(hT, bhT, tb, btb, lat_out, kpe_out, KTd, KPEd, VVd, slot):
                pb, bpb = zgroup(hT, bhT, 768, 512)
                sm, bsm = sm_r.next()
                act(junk[:, 0:512], pb[:, :], AF.Square, [bpb], [bsm], accum_out=sm[:, 0:1])
                rstd(sm[:, 0:1], KVL, [bsm], [bsm])
                c32, bc32 = ckv32_r.next()
                stt(c32[:], pb[:, :], sm[:, 0:1], gkv_r[:], ALU.mult, ALU.mult, [bpb, bsm, bgkv], [bc32])
                if lat_out is not None:
                    dmas(lat_out, c32[:], [bc32], [])
                ckvb, bckvb = ckvb_r.next()
                cp(ckvb[:], c32[:], [bc32], [bckvb], eng="act")
                pb, bpb = zgroup(hT, bhT, 1280, 64)
                act(junk[:, 0:64], pb[:, 0:64], AF.Square, [bpb], [bsm], accum_out=sm[:, 1:2])
                rstd(sm[:, 1:2], 64, [bsm], [bsm])
                kp, bkp = kp_r.next()
                stt(kp[:, 0, :], pb[:, 0:64], sm[:, 1:2], gkr_r[:], ALU.mult, ALU.mult, [bpb, bsm, bgkr], [bkp])
                tmp, btmp = tmp_r.next()
                rope(kp[:, 1:2, :], kp[:, 0:1, :], tb[:, 0:32], tb[:, 32:64], 1, 32,
                     tmp[:, 0:64].rearrange("p (h n) -> p h n", h=1), [bkp, btb, btmp], [bkp, btmp])
                if kpe_out is not None:
                    dmas(kpe_out, kp[:, 1, :], [bkp], [])
                kpb, bkpb = kpb_r.next()
                cp(kpb[:], kp[:, 1, :], [bkp], [bkpb], eng="pool")
                return (ckvb, bckvb, kpb, bkpb, KTd, KPEd, VVd, slot)

            def ret_kside(hT, bhT, tb, btb, want_rks, keep32=None):
                pb, bpb = zgroup(hT, bhT, 1856, 512)
                rkr, brkr = rkr_r.next()
                tmp, btmp = tmp_r.next()
                rope(rkr[:], pb[:, :].rearrange("p (h n) -> p h n", h=4), tb[:, 64:128], tb[:, 128:192], 4, 64,
                     tmp[:, 0:512].rearrange("p (h n) -> p h n", h=4), [bpb, btb, btmp], [brkr, btmp])
                kdec, bkdec = kdec_r.next()
                tt(kdec[:], rkr[:], bc(coef[:, C_KD:C_KD + 4], [128, 4, 128], 2), ALU.mult, [brkr, bcoef], [bkdec])
                rks = brks = None
                if want_rks:
                    rks, brks = rks_r.next()
                    ts(rks[:], rkr[:], 128 ** -0.5, None, ALU.mult, None, [brkr], [brks])
                if keep32 is not None:
                    ts(keep32[0][:], rkr[:], 128 ** -0.5, None, ALU.mult, None, [brkr], [keep32[1]])
                rvb, brvb = rvb_r.next()
                for g in range(2):
                    pv_, bpv_ = zgroup(hT, bhT, 2368 + 512 * g, 512)
                    cp(rvb[:, 2 * g:2 * g + 2, :], pv_[:, :].rearrange("p (h n) -> p h n", h=2), [bpv_], [brvb], eng="act")
                    if keep32 is not None:
                        cp(keep32[2][:, 2 * g:2 * g + 2, :], pv_[:, :].rearrange("p (h n) -> p h n", h=2), [bpv_], [keep32[3]], eng="dve")
                return kdec, bkdec, rvb, brvb, rks, brks

            def summary(kdec, bkdec, rvb, brvb):
                Ab = []
                for hp in range(2):
                    pa, bpa = bank()
                    for hh in range(2):
                        h = 2 * hp + hh
                        mm(pa[:, 256 * hh:256 * hh + 256], kdec[:, h, :], rvb[:, h, :], True, True, [bkdec, brvb], [bpa])
                    Ab.append((pa, bpa))
                return Ab

            def own_front(hT, bhT, tb, btb, u):
                pb, bpb = zgroup(hT, bhT, 1344, 512)
                rqr, brqr = rqr_r.next()
                tmp, btmp = tmp_r.next()
                rope(rqr[:], pb[:, :].rearrange("p (h n) -> p h n", h=4), tb[:, 64:128], tb[:, 128:192], 4, 64,
                     tmp[:, 0:512].rearrange("p (h n) -> p h n", h=4), [bpb, btb, btmp], [brqr, btmp])
                srg, bsrg = srg_r.next()
                for g in range(2):
                    pb, bpb = zgroup(hT, bhT, 3392 + 512 * g, 512)
                    act(srg[:, 512 * g:512 * g + 512], pb[:, :], AF.Silu, [bpb], [bsrg])
                return rqr, brqr, srg, bsrg

            def ret_post_a(src, bsrc, srg, bsrg, u):
                y32, by32 = y32_r.next()
                sm, bsm = sm_r.next()
                for h in range(4):
                    act(y32[:, h, :], src[h], AF.Identity, [bsrc[h]], [by32, bsm], accum_out=sm[:, h:h + 1])
                    act(junk[:, 256 * h:256 * h + 256], src[h], AF.Square, [bsrc[h]], [bsm], accum_out=sm[:, 4 + h:5 + h])
                ts(sm[:, 0:4], sm[:, 0:4], 1.0 / 256, None, ALU.mult, None, [bsm], [bsm])
                tt(sm[:, 8:12], sm[:, 0:4], sm[:, 0:4], ALU.mult, [bsm], [bsm])
                stt(sm[:, 4:8], sm[:, 4:8], 1.0 / 256, sm[:, 8:12], ALU.mult, ALU.subtract, [bsm], [bsm])
                act(sm[:, 4:8], sm[:, 4:8], AF.Sqrt, [bsm], [bsm], bias=EPS)
                recip(sm[:, 4:8], sm[:, 4:8], [bsm], [bsm])
                for h in range(4):
                    ts(y32[:, h, :], y32[:, h, :], sm[:, h:h + 1], sm[:, 4 + h:5 + h], ALU.subtract, ALU.mult, [by32, bsm], [by32])
                yf = y32[:].rearrange("p h n -> p (h n)")
                tt(yf, yf, gret_r[:], ALU.mult, [by32, bgret], [by32])
                retg, bretg = retg_r.next()
                tt(retg[:], yf, srg[:], ALU.mult, [by32, bsrg], [bretg])
                return (retg, bretg, u)

            def ret_post_b(retg, bretg, u):
                pb, bpb = bank()
                pv = pb[:].bitcast(BF16).rearrange("p (c n) -> p c n", c=8)
                for c in range(8):
                    tr(pv[:, c, :], retg[:, c * 128:(c + 1) * 128], ident[:], [bretg, bid], [bpb])
                retT, bretT = retT_r.next()
                cp(retT[:], pv, [bpb], [bretT], eng="act")
                dmas(RETT[:, :, u * 128:(u + 1) * 128].rearrange("c p n -> p c n"), retT[:], [bretT], [])

            def ret_post(src, bsrc, srg, bsrg, u):
                ret_post_b(*ret_post_a(src, bsrc, srg, bsrg, u))

            def ret_own(rqr, brqr, rks, brks, rvb, brvb, Sb, bSb, srg, bsrg, u):
                ret_post_b(*ro3(*ro2(*ro1(rqr, brqr, rks, brks)), rvb, brvb, Sb, bSb, srg, bsrg, u))

            def ro1(rqr, brqr, rks, brks):
                pq, bpq = bank()
                pqv = pq[:].bitcast(BF16).rearrange("p (c n) -> p c n", c=8)
                for h in range(4):
                    tr(pqv[:, h, :], rqr[:, h, :], ident[:], [brqr, bid], [bpq])
                    tr(pqv[:, 4 + h, :], rks[:, h, :], ident[:], [brks, bid], [bpq])
                rqT, brqT = rqT_r.next()
                rqdT, brqdT = rqdT_r.next()
                rkT, brkT = rkT_r.next()
                cp(rqT[:], pqv[:, 0:4, :], [bpq], [brqT], eng="act")
                tt(rqdT[:], pqv[:, 0:4, :], qdect[:], ALU.mult, [bpq, bqdec], [brqdT])
                cp(rkT[:], pqv[:, 4:8, :], [bpq], [brkT], eng="act")
                return (rqT, brqT, rqdT, brqdT, rkT, brkT)

            def ro2(rqT, brqT, rqdT, brqdT, rkT, brkT):
                ps_, bps_ = bank()
                psv = ps_[:, :].rearrange("p (h n) -> p h n", h=4)
                for h in range(4):
                    mm(psv[:, h, :], rkT[:, h, :], rqT[:, h, :], True, True, [brkT, brqT], [bps_])
                scm, bscm = scm_r.next()
                tt(scm[:], psv, DTt[:], ALU.mult, [bps_, bDT], [bscm])
                return (scm, bscm, rqdT, brqdT)

            def ro3(scm, bscm, rqdT, brqdT, rvb, brvb, Sb, bSb, srg, bsrg, u):
                src, bsrc = [], []
                for hp in range(2):
                    po, bpo = bank()
                    for hh in range(2):
                        h = 2 * hp + hh
                        o_ = po[:, 256 * hh:256 * hh + 256]
                        mm(o_, scm[:, h, :], rvb[:, h, :], True, False, [bscm, brvb], [bpo])
                        mm(o_, rqdT[:, h, :], Sb[:, h, :], False, True, [brqdT, bSb], [bpo])
                        src.append(o_)
                        bsrc.append(bpo)
                return ret_post_a(src, bsrc, srg, bsrg, u)

            def make_Sb(state, bstate):
                Sb, bSb = Sb_r.next()
                cp(Sb[:], state[:], [bstate], [bSb], eng="pool")
                return Sb, bSb

            def state_step(state, bstate, Ab):
                for h in range(4):
                    pa, bpa = Ab[h // 2]
                    stt(state[:, h, :], state[:, h, :], coef[:, C_G128 + h:C_G128 + h + 1], pa[:, 256 * (h % 2):256 * (h % 2) + 256],
                        ALU.mult, ALU.add, [bstate, bcoef, bpa], [bstate])

            def acc_add(accb, baccb, Ab, col):
                for h in range(4):
                    pa, bpa = Ab[h // 2]
                    stt(accb[:, h, :], pa[:, 256 * (h % 2):256 * (h % 2) + 256], coef[:, col + h:col + h + 1], accb[:, h, :],
                        ALU.mult, ALU.add, [baccb, bcoef, bpa], [baccb])

            tA_r = sb([128, 4, 256], F32, 1)

            def acc_add2(accb, baccb, Ab, col):
                tA, btA = tA_r.next()
                for h in range(4):
                    pa, bpa = Ab[h // 2]
                    act(tA[:, h, :], pa[:, 256 * (h % 2):256 * (h % 2) + 256], AF.Identity, [bpa, bcoef], [btA],
                        scale=coef[:, col + h:col + h + 1])
                tt(accb[:], accb[:], tA[:], ALU.add, [baccb, btA], [baccb], eng="pool")

            hT, bhT, tb, btb = front(xH2, tabH2)
            ret_kside(hT, bhT, tb, btb, False, keep32=(H2_rk32, bH2_rk32, H2_rv32, bH2_rv32))
            hT, bhT, tb, btb = front(xH, tabH)
            rqrH, brqrH, srgH, bsrgH = own_front(hT, bhT, tb, btb, UH)
            cp(H_rq32[:], rqrH[:], [brqrH], [bH_rq32])
            cp(H_srg[:], srgH[:], [bsrgH], [bH_srg], eng="pool")
            pq, bpq = bank()
            pqv = pq[:].bitcast(BF16).rearrange("p (c n) -> p c n", c=8)
            for h in range(4):
                tr(pqv[:, h, :], rqrH[:, h, :], ident[:], [brqrH, bid], [bpq])
            cp(H_rqT[:], pqv[:, 0:4, :], [bpq], [bH_rqT], eng="act")

            def phaseFa(o, p):
                T = 16 * o + p
                return front_a(xall[T], taball[T])

            def phaseZ(o, p, fr):
                hT, bhT, tb, btb = fr
                T = 16 * o + p
                own = p >= 12
                r = p - 12
                u = 4 * o + r
                kargs = latent_part(hT, bhT, tb, btb, latn[u] if own else None, kpen[u] if own else None, KT, KPE, VV, T)
                ow = own_front(hT, bhT, tb, btb, u) if own else None
                rk_ = ret_kside(hT, bhT, tb, btb, own)
                return (kargs, ow, rk_)

            def phaseY1(o, p, zz, k1args):
                kargs, ow, rk_ = zz
                own = p >= 12
                r = p - 12
                u = 4 * o + r
                if p == 0:
                    for h in range(4):
                        ts(acc_own[:, h, :], S_oct[:, h, :], coef[:, C_SO + h:C_SO + h + 1], None, ALU.mult, None, [bS_oct, bcoef], [bacc_own])
                        ts(acc_next[:, h, :], S_oct[:, h, :], coef[:, C_G2048 + h:C_G2048 + h + 1], None, ALU.mult, None, [bS_oct, bcoef], [bacc_next])
                r4 = None
                k2 = kside1(*k1args)
                kdec, bkdec, rvb, brvb, rks, brks = rk_
                Ab = summary(kdec, bkdec, rvb, brvb)
                if not own:
                    acc_add(acc_own, bacc_own, Ab, C_CO + 4 * p)
                    acc_add2(acc_next, bacc_next, Ab, C_CN + 4 * p)
                else:
                    rqr, brqr, srg, bsrg = ow
                    Sb, bSb = make_Sb(acc_own, bacc_own)
                    state_step(acc_own, bacc_own, Ab)
                    acc_add2(acc_next, bacc_next, Ab, C_CNO + 4 * r)
                    if r == 0:
                        for hp in range(2):
                            ph, bph = bank()
                            for hh in range(2):
                                h = 2 * hp + hh
                                mm(ph[:, 256 * hh:256 * hh + 256], H_rqT[:, h, :], Sb[:, h, :], True, True, [bH_rqT, bSb], [bph])
                            av = accH[:, 2 * hp:2 * hp + 2, :].rearrange("p h n -> p (h n)")
                            stt(av, ph[:, :], coef[:, C_SEL + o:C_SEL + o + 1], av, ALU.mult, ALU.add, [bph, bcoef, baccH], [baccH])
                    r4 = ro3(*ro2(*ro1(rqr, brqr, rks, brks)), rvb, brvb, Sb, bSb, srg, bsrg, u)
                if p == 15:
                    cp(S_oct[:], acc_next[:], [bacc_next], [bS_oct])
                return k2, r4

            seq = [(o, p) for o in range(NO) for p in range(16)]
            fa_q, fr_q, z_q, t_q, y_q = {}, {}, {}, {}, {}
            r4_pend = [None]
            fa_q[0] = phaseFa(*seq[0])
            fr_q[0] = front_b(*fa_q.pop(0))
            for n in range(1, len(seq) + 3):
                if n < len(seq):
                    fa_q[n] = phaseFa(*seq[n])
                if 0 <= n - 1 < len(seq):
                    z_q[n - 1] = phaseZ(*seq[n - 1], fr_q.pop(n - 1))
                if n < len(seq):
                    fr_q[n] = front_b(*fa_q.pop(n))
                if r4_pend[0] is not None:
                    ret_post_b(*r4_pend[0])
                    r4_pend[0] = None
                if 0 <= n - 2 < len(seq):
                    y_q[n - 2], r4n = phaseY1(*seq[n - 2], z_q.pop(n - 2), t_q.pop(n - 2))
                else:
                    r4n = None
                if 0 <= n - 1 < len(seq):
                    t_q[n - 1] = kside0(*z_q[n - 1][0])
                if 0 <= n - 3 < len(seq):
                    kside2(*y_q.pop(n - 3))
                r4_pend[0] = r4n
            if r4_pend[0] is not None:
                ret_post_b(*r4_pend[0])
            dmas(retp.rearrange("h d e -> d h e"), S_oct[:], [bS_oct], [])

            prod, bprod = tmp_r.next()
            pv3 = prod[:, 0:512].rearrange("p (h n) -> p h n", h=4)
            tt(pv3, H_rq32[:], H2_rk32[:], ALU.mult, [bH_rq32, bH2_rk32], [bprod])
            smh, bsmh = sm_r.next()
            red(smh[:, 0:4], pv3, [bprod], [bsmh])
            ts(smh[:, 0:4], smh[:, 0:4], coef[:, C_FLAG:C_FLAG + 1], None, ALU.mult, None, [bsmh, bcoef], [bsmh])
            for h in range(4):
                stt(accH[:, h, :], H2_rv32[:, h, :], smh[:, h:h + 1], accH[:, h, :], ALU.mult, ALU.subtract, [bH2_rv32, bsmh, baccH], [baccH])
                ts(accH[:, h, :], accH[:, h, :], coef[:, C_NINVG + h:C_NINVG + h + 1], None, ALU.mult, None, [baccH, bcoef], [baccH])
            ret_post([accH[:, h, :] for h in range(4)], [baccH] * 4, H_srg, bH_srg, UH)

            lat_r = sb([128, KVL], F32, 2)
            kpp_r = sb([128, 64], F32, 2)
            def p_load(t):
                lt, blt = lat_r.next()
                dma(lt[:], latp[t], [], [blt])
                kt_, bkt_ = kpp_r.next()
                dma(kt_[:], kpep[t], [], [bkt_])
                ckvb, bckvb = ckvb_r.next()
                cp(ckvb[:], lt[:], [blt], [bckvb], eng="pool")
                kpb, bkpb = kpb_r.next()
                cp(kpb[:], kt_[:], [bkt_], [bkpb], eng="pool")
                return (ckvb, bckvb, kpb, bkpb, KTs, KPEs, VVs, t)

            pl_q, p0_q, p1_q = {}, {}, {}
            for n in range(16 + 3):
                if n < 16:
                    pl_q[n] = p_load(n)
                if 0 <= n - 2 < 16:
                    p1_q[n - 2] = kside1(*p0_q.pop(n - 2))
                if 0 <= n - 1 < 16:
                    p0_q[n - 1] = kside0(*pl_q.pop(n - 1))
                if 0 <= n - 3 < 16:
                    kside2(*p1_q.pop(n - 3))

            dma(accS[:], S0.rearrange("h d e -> d h e"), [], [baccS])
            for h in range(4):
                ts(accS[:, h, :], accS[:, h, :], coef[:, C_CS + h:C_CS + h + 1], None, ALU.mult, None, [baccS, bcoef], [baccS])
            hT, bhT, tb, btb = front(xS, tabS)
            kargs = latent_part(hT, bhT, tb, btb, latS, kpeS, KTs, KPEs, VVs, 16)
            kside(*kargs)
            rqr, brqr, srg, bsrg = own_front(hT, bhT, tb, btb, US)
            kdec, bkdec, rvb, brvb, rks, brks = ret_kside(hT, bhT, tb, btb, True)
            Ab = summary(kdec, bkdec, rvb, brvb)
            Sb, bSb = make_Sb(accS, baccS)
            state_step(accS, baccS, Ab)
            ret_own(rqr, brqr, rks, brks, rvb, brvb, Sb, bSb, srg, bsrg, US)
            dmas(retS.rearrange("h d e -> d h e"), accS[:], [baccS], [])
            P.emit_stage()

        with contextlib.ExitStack() as st:
            cnt = [0]

            def sb(shape, dt, n=None):
                items = []
                for _ in range(n or 1):
                    cnt[0] += 1
                    items.append((st.enter_context(nc.sbuf_tensor("q%d" % cnt[0], list(shape), dt)), Buf()))
                return items[0] if n is None else Rot(items)

            banks = Rot([(st.enter_context(nc.psum_tensor("pq%d" % i, [128, 512], F32)), Buf()) for i in range(8)])
            WinQ, bWinQ = sb([128, 8, 768], BF16)
            WinG, bWinG = sb([128, 8, 2048], BF16)
            Wq, bWq = sb([128, 6, 1536], BF16)
            gcol, bgcol = sb([128, 24], F32)
            ident, bid = sb([128, 128], BF16)
            gqn_r, bgqn = sb([128, 128], F32)
            gqr_r, bgqr = sb([128, 64], F32)
            dma(ident[:], ident_d, [], [bid])
            dma(gqn_r[:], g_qn.partition_broadcast(128), [], [bgqn])
            dma(gqr_r[:], g_qr.partition_broadcast(128), [], [bgqr])
            gkn2, bgkn2 = sb([128, 128], F32)
            dma(gkn2[:], g_kn.partition_broadcast(128), [], [bgkn2])
            tt(gqn_r[:], gqn_r[:], gkn2[:], ALU.mult, [bgqn, bgkn2], [bgqn])
            P.dma(lambda: nc.sync.dma_start(out=gcol[:, 0:8], in_=g_mix.rearrange("o (c p) -> p (o c)", p=128),
                                            allow_slow_non_contiguous=True), [], [bgcol])
            P.dma(lambda: nc.sync.dma_start(out=gcol[:, 8:14], in_=g_ql.rearrange("o (c p) -> p (o c)", p=128),
                                            allow_slow_non_contiguous=True), [], [bgcol])
            with contextlib.ExitStack() as st2:
                stg = Rot([(st2.enter_context(nc.sbuf_tensor("qstg%d" % i, [128, 2048], F32)), Buf()) for i in range(10)])
                tog = [0]
                load_w(WinQ, bWinQ, w_in[:, 0:768], 8, 768, 0, [bgcol])
                load_w(WinG, bWinG, w_in[:, 4416:6464], 8, 2048, 0, [bgcol])
                load_w(Wq, bWq, w_q_up, 6, 1536, 8, [bgcol])
                P.emit_stage()
            xt_r = sb([128, D], F32, 2)
            tab_r = sb([128, 192], F32, 4)
            junk = st.enter_context(nc.sbuf_tensor("junkq", [128, D], BF16))
            sq_r = sb([128, 512], F32, 2)
            sm_r = sb([128, 64], F32, 4)
            xn_r = sb([128, D], BF16, 2)
            hT_r = sb([128, 8, 128], BF16, 2)
            tmp_r = sb([128, 1024], F32, 2)
            qln_r = sb([128, QL], BF16, 2)
            qlT_r = sb([128, 6, 128], BF16, 2)
            qn32_r = sb([128, 8, 128], F32, 2)
            qnb_r = sb([128, 8, 128], BF16, 2)
            qr32_r = sb([128, 8, 64], F32, 2)
            qrb_r = sb([128, 8, 64], BF16, 2)
            qT_r = sb([128, 8, 128], BF16, 2)
            qpT_r = sb([64, 8, 128], BF16, 2)
            gate_r = sb([128, 2048], BF16, 2)

            def bank():
                return banks.next()

            def zgroupQ(hT, bhT, c0, n):
                pb, bpb = bank()
                for c in range(8):
                    mm(pb[:, 0:n], hT[:, c, :], WinQ[:, c, c0:c0 + n], c == 0, c == 7, [bhT], [bpb])
                return pb, bpb

            def zgroupG(hT, bhT, c0, n):
                pb, bpb = bank()
                for c in range(8):
                    mm(pb[:, 0:n], hT[:, c, :], WinG[:, c, c0:c0 + n], c == 0, c == 7, [bhT], [bpb])
                return pb, bpb

            def front2(x_src, tab_src):
                xt, bxt = xt_r.next()
                dma(xt[:], x_src, [], [bxt])
                tb, btb = tab_r.next()
                dma(tb[:], tab_src, [], [btb])
                sm, bsm = sm_r.next()
                act(junk[:], xt[:], AF.Square, [bxt], [bsm], accum_out=sm[:, 0:1])
                rstd(sm[:, 0:1], D, [bsm], [bsm])
                xn, bxn = xn_r.next()
                ts(xn[:], xt[:], sm[:, 0:1], None, ALU.mult, None, [bxt, bsm], [bxn])
                pb, bpb = bank()
                pv = pb[:].bitcast(BF16).rearrange("p (c n) -> p c n", c=8)
                for c in range(8):
                    tr(pv[:, c, :], xn[:, c * 128:(c + 1) * 128], ident[:], [bxn, bid], [bpb])
                hT, bhT = hT_r.next()
                cp(hT[:], pv, [bpb], [bhT], eng="act")
                return hT, bhT, tb, btb

            def q1(hT, bhT):
                pa, bpa = zgroupQ(hT, bhT, 0, 512)
                pb2, bpb2 = zgroupQ(hT, bhT, 512, 256)
                sm, bsm = sm_r.next()
                act(junk[:, 0:512], pa[:, :], AF.Square, [bpa], [bsm], accum_out=sm[:, 0:1])
                act(junk[:, 512:768], pb2[:, 0:256], AF.Square, [bpb2], [bsm], accum_out=sm[:, 1:2])
                tt(sm[:, 0:1], sm[:, 0:1], sm[:, 1:2], ALU.add, [bsm], [bsm])
                rstd(sm[:, 0:1], QL, [bsm], [bsm])
                qln, bqln = qln_r.next()
                ts(qln[:, 0:512], pa[:, :], sm[:, 0:1], None, ALU.mult, None, [bpa, bsm], [bqln])
                ts(qln[:, 512:768], pb2[:, 0:256], sm[:, 0:1], None, ALU.mult, None, [bpb2, bsm], [bqln])
                return qln, bqln

            def gates(hT, bhT, u):
                gt, bgt = gate_r.next()
                for g in range(4):
                    pb, bpb = zgroupG(hT, bhT, 512 * g, 512)
                    act(gt[:, 512 * g:512 * g + 512], pb[:, :], AF.Sigmoid, [bpb], [bgt])
                dmas(GATE[u], gt[:], [bgt], [])

            def q2a(qln, bqln):
                pb, bpb = bank()
                pv = pb[:].bitcast(BF16).rearrange("p (c n) -> p c n", c=8)
                for c in range(6):
                    tr(pv[:, c, :], qln[:, c * 128:(c + 1) * 128], ident[:], [bqln, bid], [bpb])
                qlT, bqlT = qlT_r.next()
                cp(qlT[:], pv[:, 0:6, :], [bpb], [bqlT], eng="act")
                return qlT, bqlT

            def q2b(qlT, bqlT):
                gb = []
                sm2, bsm2 = sm_r.next()
                for g in range(4):
                    pb, bpb = bank()
                    for c in range(6):
                        mm(pb[:, 0:384], qlT[:, c, :], Wq[:, c, 384 * g:384 * g + 384], c == 0, c == 5, [bqlT], [bpb])
                    gb.append((pb, bpb))
                    pvw = pb[:, 0:384].rearrange("p (h n) -> p h n", h=2)
                    for hh in range(2):
                        act(junk[:, 192 * hh:192 * hh + 128], pvw[:, hh, 0:128], AF.Square, [bpb], [bsm2],
                            accum_out=sm2[:, 2 * g + hh:2 * g + hh + 1])
                        act(junk[:, 512 + 64 * hh:512 + 64 * hh + 64], pvw[:, hh, 128:192], AF.Square, [bpb], [bsm2],
                            accum_out=sm2[:, 8 + 2 * g + hh:8 + 2 * g + hh + 1])
                rstd(sm2[:, 0:8], 128, [bsm2], [bsm2])
                rstd(sm2[:, 8:16], 64, [bsm2], [bsm2])
                qn32, bqn32 = qn32_r.next()
                qr32, bqr32 = qr32_r.next()
                for g in range(4):
                    pb, bpb = gb[g]
                    pvw = pb[:, 0:384].rearrange("p (h n) -> p h n", h=2)
                    tt(qn32[:, 2 * g:2 * g + 2, :], pvw[:, :, 0:128], bc(sm2[:, 2 * g:2 * g + 2], [128, 2, 128], 2), ALU.mult,
                       [bpb, bsm2], [bqn32])
                    tt(qr32[:, 2 * g:2 * g + 2, :], pvw[:, :, 128:192], bc(sm2[:, 8 + 2 * g:8 + 2 * g + 2], [128, 2, 64], 2), ALU.mult,
                       [bpb, bsm2], [bqr32])
                return qn32, bqn32, qr32, bqr32

            def q3(qn32, bqn32, qr32, bqr32, tb, btb, u):
                qnb, bqnb = qnb_r.next()
                tt(qnb[:], qn32[:], bc(gqn_r[:], [128, 8, 128], 1), ALU.mult, [bqn32, bgqn], [bqnb])
                tt(qr32[:], qr32[:], bc(gqr_r[:], [128, 8, 64], 1), ALU.mult, [bqr32, bgqr], [bqr32])
                qrb, bqrb = qrb_r.next()
                tmp, btmp = tmp_r.next()
                rope(qrb[:], qr32[:], tb[:, 0:32], tb[:, 32:64], 8, 32,
                     tmp[:, 0:512].rearrange("p (h n) -> p h n", h=8), [bqr32, btb, btmp], [bqrb, btmp])
                pb, bpb = bank()
                pv = pb[:].bitcast(BF16).rearrange("p (c n) -> p c n", c=8)
                for h in range(8):
                    tr(pv[:, h, :], qnb[:, h, :], ident[:], [bqnb, bid], [bpb])
                qT, bqT = qT_r.next()
                cp(qT[:], pv, [bpb], [bqT], eng="act")
                dmas(QT[:, :, u * 128:(u + 1) * 128].rearrange("h p n -> p h n"), qT[:], [bqT], [])
                pb, bpb = bank()
                pv = pb[:].bitcast(BF16).rearrange("p (c n) -> p c n", c=8)
                for h in range(8):
                    tr(pv[0:64, h, :], qrb[:, h, :], ident[:], [bqrb, bid], [bpb])
                qpT, bqpT = qpT_r.next()
                cp(qpT[:], pv[0:64, :, :], [bpb], [bqpT], eng="act")
                dmas(QPE[:, :, u * 128:(u + 1) * 128].rearrange("h p n -> p h n"), qpT[:], [bqpT], [])

            def front2a(x_src, tab_src):
                xt, bxt = xt_r.next()
                dma(xt[:], x_src, [], [bxt])
                tb, btb = tab_r.next()
                dma(tb[:], tab_src, [], [btb])
                sm, bsm = sm_r.next()
                act(junk[:], xt[:], AF.Square, [bxt], [bsm], accum_out=sm[:, 0:1])
                rstd(sm[:, 0:1], D, [bsm], [bsm])
                xn, bxn = xn_r.next()
                ts(xn[:], xt[:], sm[:, 0:1], None, ALU.mult, None, [bxt, bsm], [bxn])
                return xn, bxn, tb, btb

            def front2b(xn, bxn, tb, btb):
                pb, bpb = bank()
                pv = pb[:].bitcast(BF16).rearrange("p (c n) -> p c n", c=8)
                for c in range(8):
                    tr(pv[:, c, :], xn[:, c * 128:(c + 1) * 128], ident[:], [bxn, bid], [bpb])
                hT, bhT = hT_r.next()
                cp(hT[:], pv, [bpb], [bhT], eng="act")
                return hT, bhT, tb, btb

            def usrc(u):
                if u == UH:
                    return xH, tabH
                if u == US:
                    return xS, tabS
                o_, r_ = divmod(u, 4)
                return xall[16 * o_ + 12 + r_], taball[16 * o_ + 12 + r_]

            fr = {0: front2b(*front2a(*usrc(0)))}
            fa = {}
            q3_pend = None
            for u in range(NU):
                hT, bhT, tb, btb = fr.pop(u)
                if u + 1 < NU:
                    fa[u + 1] = front2a(*usrc(u + 1))
                qln, bqln = q1(hT, bhT)
                gates(hT, bhT, u)
                qlT, bqlT = q2a(qln, bqln)
                if u + 1 < NU:
                    fr[u + 1] = front2b(*fa.pop(u + 1))
                if q3_pend is not None:
                    q3(*q3_pend)
                qq = q2b(qlT, bqlT)
                q3_pend = qq + (tb, btb, u)
            q3(*q3_pend)
            P.emit_stage()

        with contextlib.ExitStack() as st:
            cnt = [0]

            def sb(shape, dt, n=None):
                items = []
                for _ in range(n or 1):
                    cnt[0] += 1
                    items.append((st.enter_context(nc.sbuf_tensor("b%d" % cnt[0], list(shape), dt)), Buf()))
                return items[0] if n is None else Rot(items)

            sbanks = Rot([(st.enter_context(nc.psum_tensor("pbs%d" % i, [128, 512], F32)), Buf()) for i in range(5)])
            obanks = Rot([(st.enter_context(nc.psum_tensor("pbo%d" % i, [128, 512], F32)), Buf()) for i in range(2)])
            prbank = (st.enter_context(nc.psum_tensor("pbr", [128, 512], F32)), Buf())
            coef, bcoef = sb([128, 256], F32)
            cmask, bcm = sb([128, 128], BF16)
            maskH, bmH = sb([128, NT, 8], BF16)
            ones32, bones = sb([128, 128], F32)
            tiny, btiny = sb([128, 1], F32)
            P.op("dve", lambda: nc.vector.memset(tiny[:], 1e-30), [], [btiny])
            acc_r = sb([128, 512], F32, 2)
            kpeT, bkpeT = sb([128, (NT // 2) * 128], BF16)
            kpeTs, bkpeTs = sb([128, 9 * 128], BF16)
            dma(coef[:], coef_d, [], [bcoef])
            dma(cmask[:], cmask_d, [], [bcm])
            dma(maskH[:], maskH_d.rearrange("p (t n) -> p t n", n=8), [], [bmH])
            P.op("dve", lambda: nc.vector.memset(ones32[:], 1.0), [], [bones])
            kv_ = KPE[:, 0:NT * 128].rearrange("p (m two n) -> p m two n", two=2, n=128)
            dma(kpeT[0:64, :].rearrange("p (m n) -> p m n", n=128), kv_[:, :, 0, :], [], [bkpeT])
            dma(kpeT[64:128, :].rearrange("p (m n) -> p m n", n=128), kv_[:, :, 1, :], [], [bkpeT])
            P.op("dve", lambda: nc.vector.memset(kpeTs[:], 0.0), [], [bkpeTs])
            ks_ = KPEs[:, 0:16 * 128].rearrange("p (m two n) -> p m two n", two=2, n=128)
            dma(kpeTs[0:64, 0:8 * 128].rearrange("p (m n) -> p m n", n=128), ks_[:, :, 0, :], [], [bkpeTs])
            dma(kpeTs[64:128, 0:8 * 128].rearrange("p (m n) -> p m n", n=128), ks_[:, :, 1, :], [], [bkpeTs])
            dma(kpeTs[0:64, 8 * 128:9 * 128], KPEs[:, 16 * 128:17 * 128], [], [bkpeTs])
            kT_r = sb([128, NT * 128], BF16, 2)
            v_r = sb([128, NT, 128], BF16, 2)
            kTs_r = sb([128, 17 * 128], BF16, 2)
            vs_r = sb([128, 17, 128], BF16, 2)
            q_r = sb([128, NQ], BF16, 2)
            qp_r = sb([128, NQ], BF16, 2)
            pT_r = sb([128, 512], BF16, 6)
            ri_r = sb([128, 512], F32, 2)
            at_r = sb([128, NQ], BF16, 2)

            epi_pend = [None]
            for h in range(NH):
                kT, bkT = kT_r.next(); dma(kT[:], KT[h, :, 0:NT * 128], [], [bkT])
                vv, bvv = v_r.next(); dma(vv[:], VV[h, 0:NT].rearrange("t p n -> p t n"), [], [bvv])
                kTs, bkTs = kTs_r.next(); dma(kTs[:], KTs[h], [], [bkTs])
                vs, bvs = vs_r.next(); dma(vs[:], VVs[h].rearrange("t p n -> p t n"), [], [bvs])
                q, bq = q_r.next(); dma(q[:], QT[h], [], [bq])
                qp, bqp = qp_r.next(); dma(qp[0:64, :], QPE[h], [], [bqp]); dma(qp[64:128, :], QPE[h], [], [bqp])
                at, bat = at_r.next()

                def group(units, qc0, ncols):
                    po, bpo = obanks.next()
                    pr, bpr = prbank
                    acc, bacc = acc_r.next()
                    pend = []
                    nu = len(units)

                    def pv_stage(i, pT, bpT, c_off):
                        n = ncols - c_off
                        u_ = units[i]
                        mm(po[:, c_off:ncols], u_[2], pT[:, 0:n], i == 0, i == nu - 1, [bpT] + u_[6], [bpo])
                        if i == 0:
                            cp(acc[:, 0:ncols], pT[:, 0:n], [bpT], [bacc])
                        else:
                            tt(acc[:, c_off:ncols], acc[:, c_off:ncols], pT[:, 0:n], ALU.add, [bpT, bacc], [bacc])

                    def qk_nope(i):
                        ka, kpa, va, c_off, bias, msk, RB = units[i]
                        n = ncols - c_off
                        ps_, bps_ = sbanks.next()
                        mm(ps_[:, 0:n], ka, q[:, qc0 + c_off:qc0 + ncols], True, False, RB + [bq], [bps_])
                        return ps_, bps_

                    def qk_rope(i, ps_, bps_):
                        ka, (kpa, hf_), va, c_off, bias, msk, RB = units[i]
                        n = ncols - c_off
                        mm(ps_[:, 0:n], kpa, qp[64 * hf_:64 * hf_ + 64, qc0 + c_off:qc0 + ncols], False, True, RB + [bqp], [bps_])

                    def soft(i, ps_, bps_):
                        ka, kpa, va, c_off, bias, msk, RB = units[i]
                        n = ncols - c_off
                        pT, bpT = pT_r.next()
                        if bias is None:
                            act(pT[:, 0:n], ps_[:, 0:n], AF.Exp, [bps_], [bpT], scale=MLA_SCALE)
                        else:
                            act(pT[:, 0:n], ps_[:, 0:n], AF.Exp, [bps_, bcoef], [bpT], scale=MLA_SCALE, bias=bias)
                        if msk is not None:
                            tt(pT[:, msk[1]:msk[2]], pT[:, msk[1]:msk[2]], msk[0], ALU.mult, [bpT] + msk[3], [bpT])
                        pend.append((i, pT, bpT, c_off))

                    for i0 in range(0, nu, 2):
                        ii = [i for i in (i0, i0 + 1) if i < nu]
                        sbs = [qk_nope(i) for i in ii]
                        for i, sb_ in zip(ii, sbs):
                            qk_rope(i, *sb_)
                        for i, sb_ in zip(ii, sbs):
                            soft(i, *sb_)
                        if i0 == 4 and epi_pend[0] is not None:
                            epi_pend[0]()
                            epi_pend[0] = None
                        while len(pend) > 3:
                            pv_stage(*pend.pop(0))
                    while pend:
                        pv_stage(*pend.pop(0))
                    if epi_pend[0] is not None:
                        epi_pend[0]()
                        epi_pend[0] = None

                    def epilogue(po=po, bpo=bpo, acc=acc, bacc=bacc, at=at, bat=bat, qc0=qc0, ncols=ncols):
                        mm(pr[:, 0:ncols], ones32[:], acc[:, 0:ncols], True, True, [bacc, bones], [bpr])
                        ri, bri = ri_r.next()
                        act(ri[:, 0:ncols], pr[:, 0:ncols], AF.Ln, [bpr, btiny], [bri], bias=tiny[:, 0:1])
                        act(ri[:, 0:ncols], ri[:, 0:ncols], AF.Exp, [bri], [bri], scale=-1.0)
                        tt(at[:, qc0:qc0 + ncols], po[:, 0:ncols], ri[:, 0:ncols], ALU.mult, [bpo, bri], [bat])

                    epi_pend[0] = epilogue

                def ktile(T):
                    hf_ = T % 2
                    return (kT[:, T * 128:(T + 1) * 128], (kpeT[64 * hf_:64 * hf_ + 64, (T // 2) * 128:(T // 2 + 1) * 128], hf_), vv[:, T, :])

                for o in range(NO):
                    units = []
                    for o2 in range(o + 1):
                        for p in range(16):
                            T = 16 * o2 + p
                            ka, kpa, va = ktile(T)
                            RB = [bkT, bkpeT, bvv]
                            if o2 < o:
                                units.append((ka, kpa, va, 0, None, None, RB))
                            elif p < 12:
                                units.append((ka, kpa, va, 0, coef[:, C_VB + p:C_VB + p + 1], None, RB))
                            else:
                                r = p - 12
                                units.append((ka, kpa, va, 128 * r, None, (cmask[:], 0, 128, [bcm]), RB))
                    group(units, 512 * o, 512)
                units = []
                for T in range(NT):
                    if T % 16 >= 12:
                        continue
                    ka, kpa, va = ktile(T)
                    units.append((ka, kpa, va, 0, None, (maskH[:, T, 0:2 * NO], 0, 2 * NO, [bmH]), [bkT, bkpeT, bvv]))
                group(units, 128 * UH, 2 * NO)
                units = []
                for T in range(17):
                    units.append((kTs[:, T * 128:(T + 1) * 128], (kpeTs[64 * (T % 2):64 * (T % 2) + 64, (T // 2) * 128:(T // 2 + 1) * 128], T % 2), vs[:, T, :], 0,
                                  coef[:, C_VBS:C_VBS + 1] if T == 16 else None, None, [bkTs, bkpeTs, bvs]))
                group(units, 128 * US, 128)
                if epi_pend[0] is not None:
                    epi_pend[0]()
                    epi_pend[0] = None
                P.op("pool", lambda at=at: nc.gpsimd.memset(at[:, 128 * UH + 2 * NO:128 * UH + 128], 0.0), [], [bat])
                dmas(ATT[h], at[:], [bat], [])
            P.emit_stage()

        with contextlib.ExitStack() as st:
            cnt = [0]

            def sb(shape, dt, n=None):
                items = []
                for _ in range(n or 1):
                    cnt[0] += 1
                    items.append((st.enter_context(nc.sbuf_tensor("c%d" % cnt[0], list(shape), dt)), Buf()))
                return items[0] if n is None else Rot(items)

            banks = Rot([(st.enter_context(nc.psum_tensor("pc%d" % i, [128, 512], F32)), Buf()) for i in range(8)])
            WbA, bWbA = sb([128, 8, D], BF16)
            WbB, bWbB = sb([128, 8, D], BF16)
            Wo, bWo = sb([128, 8, D], BF16)
            stg = sb([128, 1024], F32, 10)
            ident, bid = sb([128, 128], BF16)
            dma(ident[:], ident_d, [], [bid])
            tog = [0]

            def load_w2(dst, bdst, src, KC, N, stg, gcolt=None, bg=None):
                for c in range(KC):
                    for n0 in range(0, N, 1024):
                        n1 = min(N, n0 + 1024)
                        s_, bs_ = stg.next()
                        dma(s_[:, 0:n1 - n0], src[c * 128:(c + 1) * 128, n0:n1], [], [bs_])
                        tog[0] += 1
                        if gcolt is None:
                            cp(dst[:, c, n0:n1], s_[:, 0:n1 - n0], [bs_], [], eng="act" if tog[0] % 2 else "dve")
                        elif tog[0] % 2:
                            act(dst[:, c, n0:n1], s_[:, 0:n1 - n0], AF.Identity, [bs_, bg], [], scale=gcolt[:, c:c + 1])
                        else:
                            ts(dst[:, c, n0:n1], s_[:, 0:n1 - n0], gcolt[:, c:c + 1], None, ALU.mult, None, [bs_, bg], [])
                cp(stg.items[0][0][:, 0:1], stg.items[0][0][:, 0:1], [], [it[1] for it in stg.items] + [bdst], eng="dve")

            load_w2(WbA, bWbA, w_ob[0:1024, :], 8, D, stg)
            load_w2(WbB, bWbB, w_ob[1024:2048, :], 8, D, stg)
            load_w2(Wo, bWo, w_out, 8, D, stg)
            xt_r = sb([128, D], F32, 3)
            gt_r = sb([128, 2048], BF16, 2)
            aT_r = sb([128, 8, 128], BF16, 2)
            rT_r = sb([128, 8, 128], BF16, 2)
            t32_r = sb([128, D], F32, 2)
            mg_r = sb([128, D], BF16, 2)
            mT_r = sb([128, 8, 128], BF16, 2)
            xm_r = sb([128, D], F32, 2)
            sq_r = sb([128, D], F32, 1)
            sm_r = sb([128, 8], F32, 2)
            xn_r = sb([128, D], BF16, 2)
            hn_r = sb([128, 8, 128], BF16, 2)

            def xsrc(u):
                if u == UH:
                    return xH
                if u == US:
                    return xS
                o, r = divmod(u, 4)
                return xall[16 * o + 12 + r]

            def c1_p1(u):
                xt, bxt = xt_r.next(); dma(xt[:], xsrc(u), [], [bxt])
                gt, bgt = gt_r.next(); dma(gt[:], GATE[u], [], [bgt])
                aT, baT = aT_r.next(); dma(aT[:], ATT[:, :, u * 128:(u + 1) * 128].rearrange("h p n -> p h n"), [], [baT])
                rT, brT = rT_r.next(); dma(rT[:], RETT[:, :, u * 128:(u + 1) * 128].rearrange("c p n -> p c n"), [], [brT])
                t32, bt32 = t32_r.next()
                mg, bmg = mg_r.next()
                for g in range(2):
                    pa, bpa = banks.next()
                    for c in range(8):
                        mm(pa[:, :], aT[:, c, :], WbA[:, c, 512 * g:512 * g + 512], c == 0, c == 7, [baT, bWbA], [bpa])
                    pr_, bpr_ = banks.next()
                    for c in range(8):
                        mm(pr_[:, :], rT[:, c, :], WbB[:, c, 512 * g:512 * g + 512], c == 0, c == 7, [brT, bWbB], [bpr_])
                    sl = slice(512 * g, 512 * g + 512)
                    tt(t32[:, sl], pa[:, :], gt[:, 512 * g:512 * g + 512], ALU.mult, [bpa, bgt], [bt32])
                    tt(mg[:, sl], pr_[:, :], gt[:, 1024 + 512 * g:1024 + 512 * g + 512], ALU.mult, [bpr_, bgt], [bmg])
                tt(mg[:], mg[:], t32[:], ALU.add, [bmg, bt32], [bmg])
                return (u, xt, bxt, mg, bmg)

            def c1_p2a(u, xt, bxt, mg, bmg):
                pb, bpb = banks.next()
                pv = pb[:].bitcast(BF16).rearrange("p (c n) -> p c n", c=8)
                for c in range(8):
                    tr(pv[:, c, :], mg[:, c * 128:(c + 1) * 128], ident[:], [bmg, bid], [bpb])
                mT, bmT = mT_r.next()
                cp(mT[:], pv, [bpb], [bmT], eng="act")
                return (u, xt, bxt, mT, bmT)

            def c1_p2b(u, xt, bxt, mT, bmT):
                xm, bxm = xm_r.next()
                for g in range(2):
                    po_, bpo_ = banks.next()
                    for c in range(8):
                        mm(po_[:, :], mT[:, c, :], Wo[:, c, 512 * g:512 * g + 512], c == 0, c == 7, [bmT, bWo], [bpo_])
                    tt(xm[:, 512 * g:512 * g + 512], po_[:, :], xt[:, 512 * g:512 * g + 512], ALU.add, [bpo_, bxt], [bxm])
                dmas(XMID[u], xm[:], [bxm], [])
                sq, bsq = sq_r.next()
                sm, bsm = sm_r.next()
                act(sq[:], xm[:], AF.Square, [bxm], [bsq, bsm], accum_out=sm[:, 0:1])
                rstd(sm[:, 0:1], D, [bsm], [bsm])
                xn, bxn = xn_r.next()
                ts(xn[:], xm[:], sm[:, 0:1], None, ALU.mult, None, [bxm, bsm], [bxn])
                return (u, xn, bxn)

            def c1_p3(u, xn, bxn):
                pb, bpb = banks.next()
                pv = pb[:].bitcast(BF16).rearrange("p (c n) -> p c n", c=8)
                for c in range(8):
                    tr(pv[:, c, :], xn[:, c * 128:(c + 1) * 128], ident[:], [bxn, bid], [bpb])
                hn, bhn = hn_r.next()
                cp(hn[:], pv, [bpb], [bhn], eng="act")
                dmas(HNT[:, :, u * 128:(u + 1) * 128].rearrange("c p n -> p c n"), hn[:], [bhn], [])

            p1q, p3q = {}, {}
            for n in range(NU + 2):
                if n < NU:
                    p1q[n] = c1_p1(n)
                a2 = None
                if 0 <= n - 1 < NU:
                    a2 = c1_p2a(*p1q.pop(n - 1))
                if 0 <= n - 2 < NU:
                    c1_p3(*p3q.pop(n - 2))
                if a2 is not None:
                    p3q[n - 1] = c1_p2b(*a2)
            P.emit_stage()

        with contextlib.ExitStack() as st:
            cnt = [0]

            def sbw(shape, dt):
                cnt[0] += 1
                return st.enter_context(nc.sbuf_tensor("w%d" % cnt[0], list(shape), dt)), Buf()

            Wup, bWup = sbw([128, 8, 2 * DFF], BF16)
            Wdn, bWdn = sbw([128, 22, D], BF16)
            cwT, bcwT = sbw([128, 4, NCH], F32)
            uH, buH = sbw([128, NCH, 8], F32)
            c0T, bc0T = sbw([128, NCH, 2], F32)
            with contextlib.ExitStack() as st2:
                stg = Rot([(st2.enter_context(nc.sbuf_tensor("wstg%d" % i, [128, 1024], F32)), Buf()) for i in range(10)])
                gcol2 = st2.enter_context(nc.sbuf_tensor("gcol2", [128, 8], F32)); bg2 = Buf()
                cwl = st2.enter_context(nc.sbuf_tensor("cwl", [NCH, 4, 128], F32)); bcwl = Buf()
                c0l = st2.enter_context(nc.sbuf_tensor("c0l", [NCH, 2, 128], F32)); bc0l = Buf()
                id32 = st2.enter_context(nc.sbuf_tensor("id32", [128, 128], F32)); bid32 = Buf()
                pw = st2.enter_context(nc.psum_tensor("pw", [128, 512], F32)); bpw = Buf()
                P.dma(lambda: nc.sync.dma_start(out=gcol2[:], in_=g_ffn.rearrange("o (c p) -> p (o c)", p=128),
                                                allow_slow_non_contiguous=True), [], [bg2])
                dma(id32[:], ident32_d, [], [bid32])
                dma(cwl[:, 0:3, :], cw.rearrange("k (c p) -> c k p", p=128), [], [bcwl])
                dma(cwl[:, 3:4, :], cb.rearrange("k (c p) -> c k p", p=128), [], [bcwl])
                dma(c0l[:], conv0.rearrange("k (c p) -> c k p", p=128), [], [bc0l])
                for k in range(4):
                    tr(pw[:, k * NCH:(k + 1) * NCH], cwl[:, k, :], id32[0:NCH, 0:NCH], [bcwl, bid32], [bpw])
                for k in range(2):
                    tr(pw[:, (4 + k) * NCH:(5 + k) * NCH], c0l[:, k, :], id32[0:NCH, 0:NCH], [bc0l, bid32], [bpw])
                cp(cwT[:], pw[:, 0:4 * NCH].rearrange("p (k c) -> p k c", k=4), [bpw], [bcwT])
                cp(c0T[:], pw[:, 4 * NCH:6 * NCH].rearrange("p (k c) -> p c k", k=2), [bpw], [bc0T])
                tog = [0]
                load_w2(Wup, bWup, w_up, 8, 2 * DFF, stg, gcol2, bg2)
                load_w2(Wdn, bWdn, w_dn, 22, D, stg)
                P.emit_stage()

            def sb(shape, dt, n=None):
                items = []
                for _ in range(n or 1):
                    cnt[0] += 1
                    items.append((st.enter_context(nc.sbuf_tensor("d%d" % cnt[0], list(shape), dt)), Buf()))
                return items[0] if n is None else Rot(items)

            banks = Rot([(st.enter_context(nc.psum_tensor("pd%d" % i, [128, 512], F32)), Buf()) for i in range(8)])
            coef, bcoef = sb([128, 256], F32)
            dma(coef[:], coef_d, [], [bcoef])
            hn_r = sb([128, 8, 512], BF16, 1)
            aT_r = sb([128, 22, 512], BF16, 1)
            ue_r = sb([128, 514], F32, 4)
            tm_r = sb([128, 512], F32, 6)
            xm_r = sb([128, D], F32, 2)
            y_r = sb([128, D], F32, 2)
            cst, bcst = sb([128, NCH, 2], F32)
            bW = []

            def ffn_group(col0, N, tiles, halo, cst_out, hc=0):
                hn, bhn = hn_r.next()
                dma(hn[:, :, 0:N], HNT[:, :, col0:col0 + N].rearrange("c p n -> p c n"), [], [bhn])
                aT, baT = aT_r.next()
                pend_s = [None]

                def finish_pair(tms_, i_):
                    (tg, btg), (tv, btv) = tms_
                    act(tg[:, 0:N], tg[:, 0:N], AF.Silu, [btg], [btg])
                    tt(aT[:, i_, 0:N], tg[:, 0:N], tv[:, 0:N], ALU.mult, [btg, btv], [baT])

                for i in range(22):
                    tms = []
                    ues = []
                    for k, ch in enumerate((i, 22 + i)):
                        ue, bue = ue_r.next()
                        tm, btm = tm_r.next()
                        pb, bpb = banks.next()
                        for c in range(8):
                            mm(pb[:, 0:N], Wup[:, c, ch * 128:(ch + 1) * 128], hn[:, c, 0:N], c == 0, c == 7, [bhn], [bpb])
                        if tiles is not None:
                            act(tm[:, 0:N], pb[:, 0:N], AF.Identity, [bpb], [btm], scale=cwT[:, 2, ch:ch + 1], bias=cwT[:, 3, ch:ch + 1])
                        cp(ue[:, 2:N + 2], pb[:, 0:N], [bpb], [bue], eng="act")
                        if halo is not None:
                            cp(ue[:, hc:hc + 2], halo(ch), [buH, bc0T], [bue], eng="pool")
                        if cst_out is not None:
                            cp(cst[:, ch, :], ue[:, N:N + 2], [bue], [bcst], eng="pool")
                        ues.append((ue, bue))
                        if tiles is None:
                            continue
                        stt(tm[:, 0:N], ue[:, 1:N + 1], cwT[:, 1, ch:ch + 1], tm[:, 0:N], ALU.mult, ALU.add, [bue, btm], [btm])
                        stt(tm[:, 0:N], ue[:, 0:N], cwT[:, 0, ch:ch + 1], tm[:, 0:N], ALU.mult, ALU.add, [bue, btm], [btm])
                        tms.append((tm, btm))
                    if tiles is None:
                        for k, ch in enumerate((i, 22 + i)):
                            ue, bue = ues[k]
                            tt(uH[:, ch, :], ue[:, 2:10], coef[:, C_HF:C_HF + 8], ALU.mult, [bue, bcoef], [buH])
                        continue
                    if pend_s[0] is not None:
                        finish_pair(*pend_s[0])
                    pend_s[0] = (tms, i)
                if pend_s[0] is not None:
                    finish_pair(*pend_s[0])
                    pend_s[0] = None
                if tiles is None:
                    return
                if cst_out is not None:
                    dmas(cst_out.rearrange("c p k -> p c k"), cst[:], [bcst], [])
                for t, (u, out_ap) in enumerate(tiles):
                    xm, bxm = xm_r.next()
                    dma(xm[:], XMID[u], [], [bxm])
                    y, by = y_r.next()
                    for g in range(2):
                        pb, bpb = banks.next()
                        for i in range(22):
                            mm(pb[:, :], aT[:, i, 128 * t:128 * t + 128], Wdn[:, i, 512 * g:512 * g + 512], i == 0, i == 21, [baT], [bpb])
                        tt(y[:, 512 * g:512 * g + 512], pb[:, :], xm[:, 512 * g:512 * g + 512], ALU.add, [bpb, bxm], [by])
                    dmas(out_ap, y[:], [by], [])

            P.op("dve", lambda: nc.vector.memset(uH[:], 0.0), [], [buH])
            ffn_group(128 * UH, 128, None, None, None)
            for o in range(NO):
                ffn_group(512 * o, 512, [(4 * o + r, yp[4 * o + r]) for r in range(4)],
                          (lambda ch, o=o: uH[:, ch, 2 * o:2 * o + 2]), convp if o == NO - 1 else None)
            ffn_group(128 * US, 128, [(US, ys)], (lambda ch: c0T[:, ch, :]), convS, hc=112)
            P.emit_stage()
    return nc


def _tables(pos):
    pos = np.asarray(pos, np.float32)[:, None]
    fm = (THETA ** (-np.arange(32, dtype=np.float32) / 32)).astype(np.float32)[None, :]
    fr = (THETA ** (-np.arange(64, dtype=np.float32) / 64)).astype(np.float32)[None, :]
    am = (pos * fm).astype(np.float32)
    ar = (pos * fr).astype(np.float32)
    return np.concatenate([np.cos(am), np.sin(am), np.cos(ar), np.sin(ar)], axis=1).astype(np.float32)


def _gammas():
    return (1.0 - 2.0 ** (-5.0 - np.arange(RH, dtype=np.float64)))


def _consts():
    gam = _gammas()
    idx = np.arange(128)
    DT = np.zeros((128, 4, 128), np.float32)
    qdec = np.zeros((128, 4, 128), np.float32)
    for h in range(4):
        d = idx[None, :] - idx[:, None]
        DT[:, h, :] = np.where(d >= 0, gam[h] ** np.maximum(d, 0), 0.0)
        qdec[:, h, :] = (gam[h] ** (idx + 1.0))[None, :]
    cmask = (idx[:, None] // 64 <= idx[None, :] // 64).astype(np.float32)
    return DT.reshape(128, 512), qdec.reshape(128, 512), cmask


def _coef(j, NO):
    gam = _gammas()
    c = np.zeros((128, 256), np.float64)
    for h in range(4):
        g = gam[h]
        for p in range(12):
            orig = p if p < 4 * j else p + 4
            c[:, 0 + 4 * p + h] = g ** (128.0 * (4 * j - 1 - p)) if p < 4 * j else 0.0
            c[:, 48 + 4 * p + h] = g ** (128.0 * (15 - orig))
        for r in range(4):
            c[:, 96 + 4 * r + h] = g ** (128.0 * (15 - 4 * j - r))
        c[:, 112 + h] = g ** (512.0 * j)
        c[:, 116 + h] = g ** 2048.0
        c[:, 120 + h] = g ** 128.0
        c[:, 124 + h] = g ** (-112.0)
        c[:, 128 + h] = (g ** (127.0 - np.arange(128))) * (128 ** -0.5)
        c[:, 132 + h] = np.where(np.arange(128) % 2 == 0, -1.0 / g, -1.0)
    c[:, 136] = (np.arange(128) % 2 == 0).astype(np.float64)
    for o in range(NO):
        c[2 * o:2 * o + 2, 140 + o] = 1.0
    for o in range(NO):
        c[:, 148 + 2 * o:148 + 2 * o + 2] = 0.0 if (o == 0 and j == 0) else 1.0
    for p in range(12):
        c[:, 160 + p] = 0.0 if p < 4 * j else NEG
    c[:, 172] = np.where(np.arange(128) < 112, NEG, 0.0)
    return c.astype(np.float32)


def _maskH(j, NO):
    NT = 16 * NO
    m = np.zeros((128, NT, 8), np.float32)
    for o in range(NO):
        for o2 in range(NO):
            for p in range(16):
                vis = (o2 < o) or (o2 == o and p < 12 and p < 4 * j)
                if vis:
                    m[:, 16 * o2 + p, 2 * o:2 * o + 2] = 1.0
    return m.reshape(128, NT * 8)


def host_prep(inp, NO, core):
    bf = ml_dtypes.bfloat16
    b, j = divmod(core, 4)
    xp = np.asarray(inp["x_prompt"][b], np.float32)
    xt = xp.reshape(NO, 16, 128, D)
    order = [p if p < 4 * j else p + 4 for p in range(12)] + [4 * j + r for r in range(4)]
    xall = np.ascontiguousarray(xt[:, order]).reshape(16 * NO, 128, D)
    pos = np.arange(NO * 2048).reshape(NO, 16, 128)[:, order].reshape(16 * NO, 128)
    taball = np.stack([_tables(pos[t]) for t in range(16 * NO)])
    xH = np.zeros((128, D), np.float32); xH2 = np.zeros((128, D), np.float32)
    pH = np.zeros(128); pH2 = np.zeros(128)
    for o in range(NO):
        s = 2048 * o + 512 * j
        if s >= 2:
            xH[2 * o] = xp[s - 2]; xH[2 * o + 1] = xp[s - 1]
            xH2[2 * o] = xp[s - 1]; xH2[2 * o + 1] = xp[s - 1]
            pH[2 * o] = s - 2; pH[2 * o + 1] = s - 1
            pH2[2 * o] = s - 1; pH2[2 * o + 1] = s - 1
    xS = np.zeros((128, D), np.float32)
    xS[112:] = np.asarray(inp["x_sample"][core], np.float32)
    pS = np.zeros(128); pS[112:] = PAST + np.arange(16)
    DT, qdec, cmask = _consts()
    m = {
        "xall": xall, "taball": taball, "xH": xH, "tabH": _tables(pH), "xH2": xH2, "tabH2": _tables(pH2),
        "xS": xS, "tabS": _tables(pS),
        "latp": np.ascontiguousarray(inp["cache_mla_latent"][0, core], np.float32).reshape(16, 128, KVL),
        "kpep": np.ascontiguousarray(inp["cache_mla_rope_key"][0, core], np.float32).reshape(16, 128, 64),
        "S0": np.ascontiguousarray(inp["state_retention"][0, core], np.float32),
        "conv0": np.ascontiguousarray(inp["state_ffn_conv"][0, core], np.float32),
        "ident": np.eye(128).astype(bf), "ident32": np.eye(128, dtype=np.float32),
        "cmask": cmask.astype(bf), "DT": DT, "qdec": qdec,
        "coef": _coef(j, NO), "maskH": _maskH(j, NO).astype(bf),
    }
    return m


def shared_inputs(inp):
    f = lambda k: np.ascontiguousarray(np.asarray(inp[k], np.float32)[0])
    return {
        "w_in": f("w_in"), "w_q_up": f("w_q_up"), "w_kv_up": f("w_kv_up"), "w_ob": f("w_o_branch"), "w_out": f("w_out"),
        "w_up": f("w_ffn_up"), "w_dn": f("w_ffn_down"), "cw": f("ffn_conv_w"), "cb": f("ffn_conv_b").reshape(1, -1),
        "g_mix": f("g_norm_mix").reshape(1, -1), "g_ql": f("g_q_lat").reshape(1, -1), "g_qn": f("g_q_nope").reshape(1, -1),
        "g_qr": f("g_q_rope").reshape(1, -1), "g_kv": f("g_kv_lat").reshape(1, -1), "g_kn": f("g_k_nope").reshape(1, -1),
        "g_kr": f("g_k_rope").reshape(1, -1), "g_ret": f("g_ret_out").reshape(1, -1), "g_ffn": f("g_norm_ffn").reshape(1, -1),
    }


def assemble(results, NO, nb):
    SEQ = 2048 * NO
    ncore = len(results)
    y_p = np.zeros((nb, SEQ, D), np.float32)
    lat_p = np.zeros((1, nb, SEQ, KVL), np.float32)
    kpe_p = np.zeros((1, nb, SEQ, 64), np.float32)
    ret_p = np.zeros((1, nb, RH, 128, 256), np.float32)
    conv_p = np.zeros((1, nb, 2, 2 * DFF), np.float32)
    for c in range(4 * nb):
        b, j = divmod(c, 4)
        r = results[c]
        for o in range(NO):
            s = 2048 * o + 512 * j
            y_p[b, s:s + 512] = r["yp"][4 * o:4 * o + 4].reshape(512, D)
            lat_p[0, b, s:s + 512] = r["latn"][4 * o:4 * o + 4].reshape(512, KVL)
            kpe_p[0, b, s:s + 512] = r["kpen"][4 * o:4 * o + 4].reshape(512, 64)
        if j == 3:
            ret_p[0, b] = r["retp"]
            conv_p[0, b] = r["convp"].reshape(NCH * 128, 2).T
    y_s = np.stack([results[c]["ys"][112:] for c in range(ncore)])
    lat_s = np.stack([results[c]["latS"][112:] for c in range(ncore)])[None]
    kpe_s = np.stack([results[c]["kpeS"][112:] for c in range(ncore)])[None]
    ret_s = np.stack([results[c]["retS"] for c in range(ncore)])[None]
    conv_s = np.stack([results[c]["convS"].reshape(NCH * 128, 2).T for c in range(ncore)])[None]
    return (y_p, y_s, lat_p, kpe_p, ret_p, conv_p, lat_s, kpe_s, ret_s, conv_s)


def kernel(**inputs):
    NO = 4
    nc = build(NO)
    sh = shared_inputs(inputs)
    in_maps = []
    for c in range(8):
        m = host_prep(inputs, NO, c)
        m.update(sh)
        in_maps.append(m)
    res = run_bass_kernel_spmd(nc, in_maps, core_ids=list(range(8)))
    return assemble(res.results, NO, 2)
```
